# Optimizing a Trainium2 kernel written in Bass

```python
import math
import jax, jax.numpy as jnp
from jax import lax
import numpy as np

D_MODEL = 1024
BATCH = 1
SEQ = 16384
DEPTH = 2
DEC_BATCH = 32
DEC_SEQ = 2048
PAST_LEN = 128

EXPAND = 2
D_MIX = EXPAND * D_MODEL
N_GROUPS = 4
W_GROUP = D_MIX // N_GROUPS
HEAD_DIM = 128
N_HEADS = W_GROUP // HEAD_DIM
CONV_A_WIDTH = 31
DN_CONV_WIDTH = 5
CHUNK = 64
ROPE_BASE = 10000.0
RET_DECAY_OFFSET = 5.0
NORM_EPS = 1e-6
IN_SPLIT_SIZES = (
    W_GROUP, W_GROUP, W_GROUP,
    W_GROUP, W_GROUP, W_GROUP, W_GROUP,
    W_GROUP, W_GROUP, W_GROUP, W_GROUP, W_GROUP,
    3 * W_GROUP, 2 * N_HEADS, 2 * N_HEADS, W_GROUP,
)
D_IN = sum(IN_SPLIT_SIZES)

kernel_name = "hymba_style_bidir_hybrid_encoder"

F32 = jnp.float32


def rms_norm(x, g):
    xf = x.astype(F32)
    y = xf * lax.rsqrt(jnp.mean(xf * xf, axis=-1, keepdims=True) + NORM_EPS)
    return (y * g.astype(F32)).astype(x.dtype)


def layer_norm(x, g, b):
    mu = jnp.mean(x, axis=-1, keepdims=True)
    xc = x - mu
    var = jnp.mean(xc * xc, axis=-1, keepdims=True)
    return xc * lax.rsqrt(var + NORM_EPS) * g.astype(F32) + b.astype(F32)


def head_rms_norm(o, g):
    o = o * lax.rsqrt(jnp.mean(o * o, axis=-1, keepdims=True) + NORM_EPS)
    b, h, t, d = o.shape
    return o.transpose(0, 2, 1, 3).reshape(b, t, h * d) * g.astype(F32)


def to_heads(t):
    b, s, w = t.shape
    return t.reshape(b, s, w // HEAD_DIM, HEAD_DIM).transpose(0, 2, 1, 3)


def rev(t):
    return jnp.flip(t, axis=2)


def l2_normalize(t):
    return t * lax.rsqrt(jnp.sum(t * t, axis=-1, keepdims=True) + NORM_EPS)


def depthwise_conv_centred(x, w):
    k, c = w.shape
    return lax.conv_general_dilated(
        x, w.astype(x.dtype)[:, None, :], window_strides=(1,), padding=[(k // 2, k // 2)],
        dimension_numbers=('NWC', 'WIO', 'NWC'), feature_group_count=c)


def rotary(x, pos):
    half = x.shape[-1] // 2
    inv = 1.0 / (ROPE_BASE ** (jnp.arange(half, dtype=F32) / half))
    ang = pos[:, None] * inv[None, :]
    cos, sin = jnp.cos(ang), jnp.sin(ang)
    x1, x2 = x[..., :half], x[..., half:]
    return jnp.concatenate([x1 * cos - x2 * sin, x1 * sin + x2 * cos], axis=-1)


def conformer_conv_branch(val, glu_gate, conv_w, conv_b, ln_g, ln_b):
    u = val.astype(F32) * jax.nn.sigmoid(glu_gate.astype(F32))
    u = depthwise_conv_centred(u, conv_w) + conv_b.astype(F32)
    u = layer_norm(u, ln_g, ln_b)
    return jax.nn.silu(u)


def retention_chunked(q, k, v, log_gamma):
    b, h, t, dk = q.shape
    dv = v.shape[-1]
    n, c = t // CHUNK, CHUNK
    idx = jnp.arange(c, dtype=F32)
    rel = idx[:, None] - idx[None, :]
    intra_decay = jnp.where(rel >= 0, jnp.exp(log_gamma[:, None, None] * jnp.maximum(rel, 0.0)), 0.0)
    q_decay = jnp.exp(log_gamma[:, None] * (idx + 1.0))
    k_decay = jnp.exp(log_gamma[:, None] * (c - 1.0 - idx))
    chunk_decay = jnp.exp(log_gamma * c)
    qc = q.reshape(b, h, n, c, dk)
    kc = k.reshape(b, h, n, c, dk)
    vc = v.reshape(b, h, n, c, dv)
    scores = jnp.einsum('bhnid,bhnjd->bhnij', qc, kc) * intra_decay[None, :, None]
    o_intra = jnp.einsum('bhnij,bhnje->bhnie', scores, vc)
    kv = jnp.einsum('bhnjd,bhnje->nbhde', kc * k_decay[None, :, None, :, None], vc)

    def step(s, kv_n):
        return s * chunk_decay[None, :, None, None] + kv_n, s

    _, s_prev = lax.scan(step, jnp.zeros((b, h, dk, dv), F32), kv)
    o_inter = jnp.einsum('bhnid,nbhde->bhnie', qc * q_decay[None, :, None, :, None], s_prev)
    return (o_intra + o_inter).reshape(b, h, t, dv)


def retention_branch(q, k, v, norm_g):
    t = q.shape[1]
    q, k, v = to_heads(q.astype(F32)), to_heads(k.astype(F32)), to_heads(v.astype(F32))
    pos = jnp.arange(t, dtype=F32)
    q = rotary(q, pos) * (HEAD_DIM ** -0.5)
    k = rotary(k, pos)
    log_gamma_fwd = jnp.log1p(-jnp.exp2(-RET_DECAY_OFFSET - jnp.arange(N_HEADS, dtype=F32)))
    log_gamma_bwd = log_gamma_fwd[::-1]
    o = retention_chunked(q, k, v, log_gamma_fwd) + rev(retention_chunked(rev(q), rev(k), rev(v), log_gamma_bwd))
    return head_rms_norm(o, norm_g)


def chunk_gla(q, k, v, log_f):
    b, h, t, dk = q.shape
    dv = v.shape[-1]
    n, c = t // CHUNK, CHUNK

    def chunks(a):
        return jnp.moveaxis(a.reshape(b, h, n, c, a.shape[-1]), 2, 0)

    cum = jnp.cumsum(log_f.reshape(b, h, n, c, dk), axis=3)
    tri = jnp.tril(jnp.ones((c, c), dtype=bool))

    def step(s, inp):
        q_n, k_n, v_n, b_n = inp
        b_last = b_n[:, :, -1, :]
        decay = jnp.exp(jnp.where(tri[:, :, None], b_n[:, :, :, None, :] - b_n[:, :, None, :, :], -jnp.inf))
        attn = jnp.einsum('bhid,bhjd,bhijd->bhij', q_n, k_n, decay)
        o_n = jnp.einsum('bhid,bhde->bhie', q_n * jnp.exp(b_n), s) + jnp.einsum('bhij,bhje->bhie', attn, v_n)
        s = s * jnp.exp(b_last)[..., None] + jnp.einsum('bhjd,bhje->bhde', k_n * jnp.exp(b_last[:, :, None, :] - b_n), v_n)
        return s, o_n

    _, o = lax.scan(step, jnp.zeros((b, h, dk, dv), F32), (chunks(q), chunks(k), chunks(v), jnp.moveaxis(cum, 2, 0)))
    return jnp.moveaxis(o, 0, 2).reshape(b, h, t, dv)


def hgrn2_branch(q, f_fwd, f_bwd, i, lb, norm_g):
    q = to_heads(jax.nn.silu(q.astype(F32))) * (HEAD_DIM ** -0.5)
    v = to_heads(i.astype(F32))

    def forget(pre, lb_dir):
        log_f = jnp.logaddexp(jnp.log(lb_dir), jnp.log1p(-lb_dir) + jax.nn.log_sigmoid(pre.astype(F32)))
        log_f = to_heads(log_f)
        return -jnp.expm1(log_f), log_f

    k_f, g_f = forget(f_fwd, lb[0])
    k_b, g_b = forget(f_bwd, lb[1])
    o = chunk_gla(q, k_f, v, g_f) + rev(chunk_gla(rev(q), rev(k_b), rev(v), rev(g_b)))
    return head_rms_norm(o, norm_g)


def chunk_gated_delta(q, k, v, g, beta):
    b, h, t, dk = q.shape
    dv = v.shape[-1]
    n, c = t // CHUNK, CHUNK
    qc = q.reshape(b, h, n, c, dk)
    kc = k.reshape(b, h, n, c, dk)
    vc = v.reshape(b, h, n, c, dv)
    bc = beta.reshape(b, h, n, c)[..., None]
    cum = jnp.cumsum(g.reshape(b, h, n, c), axis=-1)
    tri = jnp.tril(jnp.ones((c, c), dtype=bool))
    strict = jnp.tril(jnp.ones((c, c), dtype=bool), -1)
    decay = jnp.exp(jnp.where(tri, cum[..., :, None] - cum[..., None, :], -jnp.inf))
    k_beta = kc * bc
    v_beta = vc * bc
    a_mat = jnp.where(strict, jnp.einsum('bhnid,bhnjd->bhnij', k_beta, kc) * decay, 0.0)
    t_mat = a_mat + jnp.eye(c, dtype=F32)
    u = lax.linalg.triangular_solve(t_mat, v_beta, left_side=True, lower=True, unit_diagonal=True)
    w = lax.linalg.triangular_solve(t_mat, k_beta * jnp.exp(cum)[..., None], left_side=True, lower=True, unit_diagonal=True)
    qk = jnp.einsum('bhnid,bhnjd->bhnij', qc, kc) * decay
    q_dec = qc * jnp.exp(cum)[..., None]
    k_dec = kc * jnp.exp(cum[..., -1:] - cum)[..., None]
    last = jnp.exp(cum[..., -1])

    def step(s, inp):
        u_n, w_n, qk_n, qd_n, kd_n, last_n = inp
        v_new = u_n - jnp.einsum('bhid,bhde->bhie', w_n, s)
        o_n = jnp.einsum('bhid,bhde->bhie', qd_n, s) + jnp.einsum('bhij,bhje->bhie', qk_n, v_new)
        s = s * last_n[..., None, None] + jnp.einsum('bhjd,bhje->bhde', kd_n, v_new)
        return s, o_n

    xs = (jnp.moveaxis(u, 2, 0), jnp.moveaxis(w, 2, 0), jnp.moveaxis(qk, 2, 0),
          jnp.moveaxis(q_dec, 2, 0), jnp.moveaxis(k_dec, 2, 0), jnp.moveaxis(last, 2, 0))
    _, o = lax.scan(step, jnp.zeros((b, h, dk, dv), F32), xs)
    return jnp.moveaxis(o, 0, 2).reshape(b, h, t, dv)


def gated_deltanet_branch(qkv, a_pre, beta_pre, conv_w, a_log, dt_bias, norm_g):
    bsz, t, _ = qkv.shape
    qkv = jax.nn.silu(depthwise_conv_centred(qkv.astype(F32), conv_w))
    q, k, v = jnp.split(qkv, 3, axis=-1)
    q = l2_normalize(to_heads(q)) * (HEAD_DIM ** -0.5)
    k = l2_normalize(to_heads(k))
    v = to_heads(v)
    a_pre = a_pre.astype(F32).reshape(bsz, t, 2, N_HEADS)
    beta = jax.nn.sigmoid(beta_pre.astype(F32).reshape(bsz, t, 2, N_HEADS))
    g = -jnp.exp(a_log.astype(F32)) * jax.nn.softplus(a_pre + dt_bias.astype(F32))
    g = g.transpose(2, 0, 3, 1)
    beta = beta.transpose(2, 0, 3, 1)
    o = chunk_gated_delta(q, k, v, g[0], beta[0]) + rev(chunk_gated_delta(rev(q), rev(k), rev(v), rev(g[1]), rev(beta[1])))
    return head_rms_norm(o, norm_g)


def hybrid_layer(x, c, ada_w, ada_b, norm_g, w_in, conv_a_w, conv_a_b, ln_a_g, ln_a_b, ret_norm_g,
                 hgrn_lb, hgrn_norm_g, dn_conv_w, dn_a_log, dn_dt_bias, dn_norm_g, w_out):
    mod = (jax.nn.silu(c) @ ada_w + ada_b)[:, None, :]
    shift, scale, gate = jnp.split(mod, 3, axis=-1)
    h = rms_norm(x, norm_g) * (1.0 + scale) + shift
    proj = h @ w_in
    points = [int(p) for p in np.cumsum(IN_SPLIT_SIZES)[:-1]]
    (a_val, a_glu, a_z, b_q, b_k, b_v, b_z, c_q, c_ff, c_fb, c_i, c_z,
     d_qkv, d_a, d_beta, d_z) = jnp.split(proj, points, axis=-1)
    o_a = conformer_conv_branch(a_val, a_glu, conv_a_w, conv_a_b, ln_a_g, ln_a_b)
    o_b = retention_branch(b_q, b_k, b_v, ret_norm_g)
    o_c = hgrn2_branch(c_q, c_ff, c_fb, c_i, hgrn_lb, hgrn_norm_g)
    o_d = gated_deltanet_branch(d_qkv, d_a, d_beta, dn_conv_w, dn_a_log, dn_dt_bias, dn_norm_g)
    mixed = jnp.concatenate([
        o_a * jax.nn.silu(a_z.astype(F32)),
        o_b * jax.nn.silu(b_z.astype(F32)),
        o_c * jax.nn.silu(c_z.astype(F32)),
        o_d * jax.nn.silu(d_z.astype(F32)),
    ], axis=-1)
    out = mixed.astype(x.dtype) @ w_out
    return x + (gate * out).astype(x.dtype)


def setup_inputs(seed: int = 0) -> dict:
    key = jax.random.key(seed)
    ks = jax.random.split(key, 24)

    def nrm(k, shape, s):
        return jax.random.normal(k, shape, F32) * s

    x_prompt = nrm(ks[0], (BATCH, SEQ, D_MODEL), 1.0)
    x_sample = nrm(ks[1], (DEC_BATCH, DEC_SEQ, D_MODEL), 1.0)
    c_prompt = nrm(ks[2], (BATCH, D_MODEL), 1.0)
    c_sample = nrm(ks[3], (DEC_BATCH, D_MODEL), 1.0)
    ada_w = nrm(ks[4], (DEPTH, D_MODEL, 3 * D_MODEL), 0.5 * D_MODEL ** -0.5)
    ada_b = nrm(ks[5], (DEPTH, 3 * D_MODEL), 0.02)
    norm_g = 1.0 + nrm(ks[6], (DEPTH, D_MODEL), 0.02)
    w_in = nrm(ks[7], (DEPTH, D_MODEL, D_IN), D_MODEL ** -0.5)
    conv_a_w = nrm(ks[8], (DEPTH, CONV_A_WIDTH, W_GROUP), CONV_A_WIDTH ** -0.5)
    conv_a_b = nrm(ks[9], (DEPTH, W_GROUP), 0.02)
    ln_a_g = 1.0 + nrm(ks[10], (DEPTH, W_GROUP), 0.02)
    ln_a_b = nrm(ks[11], (DEPTH, W_GROUP), 0.02)
    ret_norm_g = 1.0 + nrm(ks[12], (DEPTH, W_GROUP), 0.02)
    hgrn_lb_logits = nrm(ks[13], (DEPTH, 2, W_GROUP), 0.5)
    hgrn_norm_g = 1.0 + nrm(ks[14], (DEPTH, W_GROUP), 0.02)
    dn_conv_w = nrm(ks[15], (DEPTH, DN_CONV_WIDTH, 3 * W_GROUP), DN_CONV_WIDTH ** -0.5)
    dn_a_log = jnp.log(jax.random.uniform(ks[16], (DEPTH, 2, N_HEADS), F32, 1.0, 16.0))
    dt = jnp.exp(jax.random.uniform(ks[17], (DEPTH, 2, N_HEADS), F32, math.log(1e-3), math.log(1e-1)))
    dn_dt_bias = dt + jnp.log(-jnp.expm1(-dt))
    dn_norm_g = 1.0 + nrm(ks[18], (DEPTH, W_GROUP), 0.02)
    w_out = nrm(ks[19], (DEPTH, D_MIX, D_MODEL), D_MIX ** -0.5)
    final_g = 1.0 + nrm(ks[20], (D_MODEL,), 0.02)
    return {
        "x_prompt": x_prompt, "x_sample": x_sample, "c_prompt": c_prompt, "c_sample": c_sample,
        "ada_w": ada_w, "ada_b": ada_b, "norm_g": norm_g, "w_in": w_in,
        "conv_a_w": conv_a_w, "conv_a_b": conv_a_b, "ln_a_g": ln_a_g, "ln_a_b": ln_a_b,
        "ret_norm_g": ret_norm_g, "hgrn_lb_logits": hgrn_lb_logits, "hgrn_norm_g": hgrn_norm_g,
        "dn_conv_w": dn_conv_w, "dn_a_log": dn_a_log, "dn_dt_bias": dn_dt_bias, "dn_norm_g": dn_norm_g,
        "w_out": w_out, "final_g": final_g,
    }


def reference(x_prompt, x_sample, c_prompt, c_sample, ada_w, ada_b, norm_g, w_in, conv_a_w, conv_a_b,
              ln_a_g, ln_a_b, ret_norm_g, hgrn_lb_logits, hgrn_norm_g, dn_conv_w, dn_a_log, dn_dt_bias,
              dn_norm_g, w_out, final_g):
    lb = jnp.cumsum(jax.nn.softmax(hgrn_lb_logits.astype(F32), axis=0), axis=0)
    lb = lb - lb[:1]

    def trunk(x, c):
        for l in range(DEPTH):
            x = hybrid_layer(x, c, ada_w[l], ada_b[l], norm_g[l], w_in[l], conv_a_w[l], conv_a_b[l],
                             ln_a_g[l], ln_a_b[l], ret_norm_g[l], lb[l], hgrn_norm_g[l], dn_conv_w[l],
                             dn_a_log[l], dn_dt_bias[l], dn_norm_g[l], w_out[l])
        return rms_norm(x, final_g)

    y_prompt = trunk(x_prompt, c_prompt)
    y_sample = trunk(x_sample, c_sample)
    return (y_prompt, y_sample)
```

```python
import numpy as np
import concourse.bass as bass
import concourse.mybir as mybir
from concourse.bass_utils import run_bass_kernel_spmd

F32 = mybir.dt.float32
BF16 = mybir.dt.bfloat16
AF = mybir.ActivationFunctionType
ALU = mybir.AluOpType
AX = mybir.AxisListType

D = 1024
DEPTH = 2
DMIX = 2048
WG = 512
D_IN = 8208
EPS = 1e-6
NCORES = 8
KA = 31
HALO = 15


class Prog:
    COMPUTE = ("pe", "dve", "act", "pool")
    NDMASEM = 24

    def __init__(self, nc):
        self.nc = nc
        self.ops = []
        self.state = {}
        self.ndma = 0
        self.expand = {}

    FIXED = {"ident", "identb", "onesb", "onesh", "finalg", "tmpA", "shift", "gmod", "gate", "psb", "sum_in", "sum_out", "hT32"}

    def _x(self, keys):
        out = []
        for k in keys:
            if isinstance(k, str) and k not in self.expand and k not in self.FIXED:
                raise KeyError("unknown resource key %r" % (k,))
            out.extend(self.expand.get(k, [k]))
        return out

    def _deps(self, reads, writes):
        deps = set()
        for k in list(reads) + list(writes):
            st = self.state.get(k)
            if st and st[0] is not None:
                deps.add(st[0])
        for k in writes:
            st = self.state.get(k)
            if st:
                deps.update(st[1])
        return deps

    def _update(self, idx, reads, writes):
        for k in reads:
            st = self.state.setdefault(k, [None, []])
            st[1].append(idx)
        for k in writes:
            self.state[k] = [idx, []]

    def op(self, eng, fn, reads=(), writes=()):
        reads, writes = self._x(reads), self._x(writes)
        deps = self._deps(reads, writes)
        idx = len(self.ops)
        self.ops.append(dict(eng=eng, fn=fn, deps=deps, kind="c"))
        self._update(idx, reads, writes)
        return idx

    def dma(self, out, in_, reads=(), writes=(), eng="sp", **kw):
        reads, writes = self._x(reads), self._x(writes)
        deps = self._deps(reads, writes)
        idx = len(self.ops)
        self.ops.append(dict(eng=eng, fn=lambda e: e.dma_start(out=out, in_=in_, **kw), deps=deps,
                             kind="d", dn=self.ndma))
        self.ndma += 1
        self._update(idx, reads, writes)
        return idx

    def emit(self, sems, dsems):
        nc = self.nc
        engs = {"pe": nc.tensor, "dve": nc.vector, "act": nc.scalar, "pool": nc.gpsimd, "sp": nc.sync}
        ops = self.ops
        last_dma = {}
        for i, o in enumerate(ops):
            if o["kind"] == "d":
                slot = o["dn"] % self.NDMASEM
                if slot in last_dma:
                    o["deps"].add(last_dma[slot])
                last_dma[slot] = i
                o["slot"] = slot
        needed = set()
        for i, o in enumerate(ops):
            for d in o["deps"]:
                od = ops[d]
                if od["kind"] == "c" and od["eng"] == "pe" and o["eng"] == "pe" and o["kind"] == "c":
                    continue
                needed.add(d)
        cnt = {e: 0 for e in self.COMPUTE}
        dcnt = [0] * self.NDMASEM
        val = {}
        waited = {}
        for i, o in enumerate(ops):
            e = engs[o["eng"]]
            for d in sorted(o["deps"]):
                od = ops[d]
                if od["kind"] == "c":
                    if od["eng"] == "pe" and o["eng"] == "pe" and o["kind"] == "c":
                        continue
                    sem, v = sems[od["eng"]], val[d]
                    key = (o["eng"], od["eng"])
                else:
                    sem, v = dsems[od["slot"]], val[d]
                    key = (o["eng"], "d%d" % od["slot"])
                if waited.get(key, 0) >= v:
                    continue
                waited[key] = v
                e.wait_ge(sem, v)
            ins = o["fn"](e)
            if o["kind"] == "d":
                dcnt[o["slot"]] += 16
                val[i] = dcnt[o["slot"]]
                ins.then_inc(dsems[o["slot"]], 16)
            elif i in needed:
                cnt[o["eng"]] += 1
                val[i] = cnt[o["eng"]]
                ins.then_inc(sems[o["eng"]], 1)
        self.final_dma = [(dsems[s], dcnt[s]) for s in range(self.NDMASEM) if dcnt[s]]


def ret_consts():
    h = np.arange(4, dtype=np.float64)
    lgf = np.log1p(-np.exp2(-5.0 - h))
    lgb = lgf[::-1].copy()
    sc = 128.0 ** -0.5
    j = np.arange(128, dtype=np.float64)
    out = np.zeros((7, 128, 4, 128), dtype=np.float64)
    for hh in range(4):
        dij = j[None, :] - j[:, None]
        m = np.where(dij > 0, np.exp(lgf[hh] * np.maximum(dij, 0)), 0.0) \
            + np.where(dij < 0, np.exp(lgb[hh] * np.maximum(-dij, 0)), 0.0) + np.where(dij == 0, 2.0, 0.0)
        out[0, :, hh, :] = sc * m
        out[1, :, hh, :] = sc * np.exp(lgf[hh] * (j + 1.0))[None, :]
        out[2, :, hh, :] = sc * np.exp(lgb[hh] * (128.0 - j))[None, :]
        out[3, :, hh, :] = np.exp(lgf[hh] * (127.0 - j))[:, None]
        out[4, :, hh, :] = np.exp(lgb[hh] * j)[:, None]
        out[5, :, hh, :] = np.exp(lgf[hh] * 128.0)
        out[6, :, hh, :] = np.exp(lgb[hh] * 128.0)
    return np.ascontiguousarray(out.reshape(7, 128, 512).transpose(1, 0, 2)).astype(np.float32)


def ret_seg_decay(T):
    h = np.arange(4, dtype=np.float64)
    lgf = np.log1p(-np.exp2(-5.0 - h))
    lgb = lgf[::-1].copy()
    out = np.zeros((128, 2, 4, 128), dtype=np.float64)
    for hh in range(4):
        out[:, 0, hh, :] = np.exp(lgf[hh] * T)
        out[:, 1, hh, :] = np.exp(lgb[hh] * T)
    return np.ascontiguousarray(out.reshape(128, 2, 512)).astype(np.float32)


def dn_masks():
    a = np.arange(128)
    same = (a[:, None] // 64) == (a[None, :] // 64)
    triL = same & (a[:, None] <= a[None, :])
    triU = same & (a[:, None] >= a[None, :])
    sL = same & (a[:, None] > a[None, :])
    sU = same & (a[:, None] < a[None, :])
    return np.ascontiguousarray(np.stack([triL, triU, sL, sU], axis=1).astype(np.float32))


def gla_masks():
    j = np.arange(128)
    same = (j[:, None] // 32) == (j[None, :] // 32)
    fwd = same & (j[:, None] <= j[None, :])
    bwd = same & (j[:, None] >= j[None, :])
    return np.ascontiguousarray(np.stack([fwd, bwd], axis=1).astype(np.float32))


def rope_tables(pos0, T):
    half = 64
    inv = (1.0 / (10000.0 ** (np.arange(half, dtype=np.float32) / np.float32(half)))).astype(np.float32)
    pos = (np.arange(T, dtype=np.float32) + np.float32(pos0)).astype(np.float32)
    ang = (pos[:, None] * inv[None, :]).astype(np.float32)
    cos = np.cos(ang).astype(np.float32).T
    sin = np.sin(ang).astype(np.float32).T
    cosT = np.concatenate([cos, cos], axis=0)
    sinT = np.concatenate([-sin, sin], axis=0)
    return np.ascontiguousarray(np.stack([cosT, sinT], axis=0))


def build(T, NSEG, LAYERS=DEPTH, MIXERS="ABCD", rope_idx=None, NR=0, PSEG=None):
    NRP = max(NR, 1)
    NB = T // 128
    NT = T // 512
    if rope_idx is None:
        rope_idx = [0] * NSEG
    nc = bass.Bass("TRN2", target_bir_lowering=False)

    def din(name, shape, dt=F32):
        return nc.dram_tensor(name, list(shape), dt, kind="ExternalInput").ap()

    x_in = din("x_in", [NSEG, T, D])
    cT = din("cT", [D, NSEG])
    ada_w = din("ada_w", [DEPTH, D, 3 * D])
    ada_b = din("ada_b", [DEPTH, 3 * D])
    norm_g = din("norm_g", [DEPTH, D])
    w_in = din("w_in", [DEPTH, D, D_IN])
    w_out = din("w_out", [DEPTH, DMIX, D])
    conv_a_wT = din("conv_a_wT", [DEPTH, WG, KA])
    conv_a_b = din("conv_a_b", [DEPTH, WG])
    ln_a_g = din("ln_a_g", [DEPTH, WG])
    ln_a_b = din("ln_a_b", [DEPTH, WG])
    ret_norm_g = din("ret_norm_g", [DEPTH, WG])
    hgrn_norm_g = din("hgrn_norm_g", [DEPTH, WG])
    hgrn_lb_logits = din("hgrn_lb_logits", [DEPTH, 2, WG])
    gmask = din("gmask", [128, 2, 128])
    dn_conv_wT = din("dn_conv_wT", [DEPTH, 3 * WG, 5])
    dn_a_log = din("dn_a_log", [DEPTH, 2, 4])
    dn_dt_bias = din("dn_dt_bias", [DEPTH, 2, 4])
    dn_norm_g = din("dn_norm_g", [DEPTH, WG])
    dconst = din("dconst", [128, 4, 128])
    oh_in = din("oh", [8])
    xp_full = din("xp_full", [NRP, T, D])
    ohLR_in = din("ohLR", [2, 8])
    ropeP = din("ropeP", [NRP, 2, 128, T])
    retT_in = din("retT", [128, 2, 512])
    final_g = din("final_g", [D])
    ident_in = din("ident", [128, 128])
    rope = din("rope", [2, 2, 128, T])
    retc = din("retc", [128, 7, 512])
    y_out = nc.dram_tensor("y_out", [NSEG, T, D], F32, kind="ExternalOutput").ap()

    def scr(name, shape, dt):
        return nc.dram_tensor(name, list(shape), dt, kind="Internal").ap()

    wbf_in = scr("wbf_in", [DEPTH, D, D_IN], BF16)
    wbf_out = scr("wbf_out", [DEPTH, DMIX, D], BF16)
    wbf_ada = scr("wbf_ada", [DEPTH, D, 3 * D], BF16)
    mod_scr = scr("mod_scr", [DEPTH, NSEG, 3 * D], F32)
    x_scr = scr("x_scr", [NSEG, T, D], F32)
    x_acc = scr("x_acc", [NSEG, T, D], F32)
    d_scr = scr("d_scr", [2, NB, 128, 512], F32)
    cum_scr = scr("cum_scr", [2, NB, 2, 4, 128], F32)
    WS = 4112
    NRr = max(NR, 1)
    sum_in = scr("sum_in", [128, WS], F32)
    sum_out = scr("sum_out", [NRr * 128, WS], F32)
    sin_scr = scr("sin_scr", [6, 128, 512], F32)
    sv = scr("sv", [2, NRP + 1, 3, 128, 512], F32)
    xp1 = scr("xp1", [NRP, T, D], F32)

    P = Prog(nc)
    from contextlib import ExitStack
    es = ExitStack()

    def sb(name, shape, dt=F32):
        return es.enter_context(nc.sbuf_tensor(name, list(shape), dt))

    hT = sb("hT", [128, 8, T], BF16)
    mixedT = sb("mixedT", [128, 4, T], BF16)
    Wb = [sb("Wb%d" % i, [128, 8, 512], BF16) for i in range(2)]
    xin = [sb("xin%d" % i, [128, D], F32) for i in range(2)]
    htm = [sb("htm0", [128, D], BF16)] * 2
    xo = [sb("xo0", [128, D], F32)] * 2
    tmpA = sb("tmpA", [128, D], F32)
    shift_bc = sb("shift_bc", [128, D], F32)
    gmod_bc = sb("gmod_bc", [128, D], F32)
    gate_bc = sb("gate_bc", [128, D], F32)
    normg_bc = tmpA
    finalg_bc = sb("finalg_bc", [128, D], F32)
    ident = sb("ident_sb", [128, 128], F32)
    identb = sb("identb", [128, 128], BF16)
    onesb = sb("onesb", [128, 128], BF16)
    onesh = sb("onesh", [128, 128], BF16)
    small = sb("small", [128, 64], F32)
    hT32 = sb("hT32", [128, 8, 32], BF16)
    AW = 27136
    arena = sb("arena", [128, AW], F32)

    class Arena:
        def __init__(self):
            self.off = 0
            self.tab = {}

        def reset(self):
            self.off = 0
            self.tab = {}

        def get(self, name, shape, dt=F32, nsub=0, alias=None):
            esz = 4 if dt == F32 else 2
            nbytes = int(np.prod(shape[1:])) * esz
            nwords = (nbytes + 3) // 4
            if alias is not None:
                off = self.tab[alias]
            else:
                off = self.off
                self.off = off + ((nwords + 127) // 128) * 128
                assert self.off <= AW, ("arena overflow", name, self.off)
            self.tab[name] = off
            ap = arena[0:shape[0], off:off + nwords]
            if dt != F32:
                ap = ap.bitcast(dt)
            ap = ap[:, 0:int(np.prod(shape[1:]))]
            if len(shape) == 3:
                ap = ap.rearrange("p (a b) -> p a b", a=shape[1])
            elif len(shape) == 4:
                ap = ap.rearrange("p (a b c) -> p a b c", a=shape[1], b=shape[2])
            g0, g1 = (off * 4) // 512, (off * 4 + nbytes - 1) // 512
            P.expand[name] = [("g", g) for g in range(g0, g1 + 1)]
            if nsub:
                sz = nbytes // nsub
                for i in range(nsub):
                    a0, a1 = (off * 4 + i * sz) // 512, (off * 4 + (i + 1) * sz - 1) // 512
                    P.expand[(name, i)] = [("g", g) for g in range(a0, a1 + 1)]
            return ap

    AR = Arena()

    ps = [es.enter_context(nc.psum_tensor("ps%d" % i, [128, 512], F32)) for i in range(7)]
    psb = es.enter_context(nc.psum_tensor("psb", [128, 1024], BF16))
    sem_names = ["pe", "dve", "act", "pool"]
    sems = {n: es.enter_context(nc.semaphore("s_" + n)) for n in sem_names}
    dsems = [es.enter_context(nc.semaphore("d%d" % i)) for i in range(Prog.NDMASEM)]

    rr = {"ps": 0, "cast": 0, "w": 0, "x": 0}

    def next_ps():
        b = rr["ps"] % 7
        rr["ps"] += 1
        return b

    P.dma(ident[:], ident_in[:, :], writes=["ident"])
    P.op("dve", lambda e: e.tensor_copy(out=identb[:], in_=ident[:]), reads=["ident"], writes=["identb"])
    P.op("pool", lambda e: e.memset(onesb[:], 1.0 / 512.0), writes=["onesb"])
    P.op("pool", lambda e: e.memset(onesh[:], 1.0 / 128.0), writes=["onesh"])
    P.dma(finalg_bc[:], final_g.partition_broadcast(128), writes=["finalg"])

    AR.reset()
    castst = [AR.get("castst%d" % i, [128, 1026], F32) for i in range(2)]
    castbf = [AR.get("castbf%d" % i, [128, 1026], BF16) for i in range(2)]
    scT = AR.get("scT", [128, 8, NSEG], BF16)
    cTs = AR.get("cTs", [128, 8, NSEG], F32)
    modrow = AR.get("modrow", [NSEG, 512], F32)
    adab_bc = AR.get("adab_bc", [NSEG, 512], F32)

    def cast_dram(src, dst, R, C, tag):
        ctile = 1026 if C % 1026 == 0 else 1024
        for r0 in range(0, R, 128):
            for c0 in range(0, C, ctile):
                i = rr["cast"] % 2
                rr["cast"] += 1
                P.dma(castst[i][:, 0:ctile], src[r0:r0 + 128, c0:c0 + ctile], writes=["castst%d" % i])
                engn = ("dve", "pool", "act")[rr["cast"] % 3]
                if engn == "act":
                    P.op("act", lambda e, i=i: e.copy(out=castbf[i][:, 0:ctile], in_=castst[i][:, 0:ctile]),
                         reads=["castst%d" % i], writes=["castbf%d" % i])
                else:
                    P.op(engn, lambda e, i=i: e.tensor_copy(out=castbf[i][:, 0:ctile], in_=castst[i][:, 0:ctile]),
                         reads=["castst%d" % i], writes=["castbf%d" % i])
                P.dma(dst[r0:r0 + 128, c0:c0 + ctile], castbf[i][:, 0:ctile], reads=["castbf%d" % i], writes=[tag])

    for l in range(LAYERS):
        cast_dram(w_in[l], wbf_in[l], D, D_IN, ("wbf_in", l))
        cast_dram(w_out[l], wbf_out[l], DMIX, D, ("wbf_out", l))
        cast_dram(ada_w[l], wbf_ada[l], D, 3 * D, ("wbf_ada", l))

    def load_w(src, col0, ncols, rows0, nk, tag):
        i = rr["w"] % 2
        rr["w"] += 1
        P.dma(Wb[i][:, 0:nk, 0:ncols],
              src[rows0:rows0 + nk * 128, col0:col0 + ncols].rearrange("(kc p) n -> p kc n", p=128),
              reads=[tag], writes=[("Wb", i)])
        return i

    P.dma(cTs[:], cT.rearrange("(kc p) s -> p kc s", p=128), writes=["cTs"], allow_slow_non_contiguous=True)
    P.op("act", lambda e: e.activation(out=scT[:], in_=cTs[:], func=AF.Silu), reads=["cTs"], writes=["scT"])
    for l in range(LAYERS):
        for g in range(6):
            P.dma(adab_bc[:], ada_b[l, g * 512:(g + 1) * 512].partition_broadcast(NSEG), writes=["adab_bc"])
            wi = load_w(wbf_ada[l], g * 512, 512, 0, 8, ("wbf_ada", l))
            b = next_ps()
            for kc in range(8):
                P.op("pe", lambda e, kc=kc, wi=wi, b=b: e.matmul(ps[b][0:NSEG, :], lhsT=scT[:, kc, :], rhs=Wb[wi][:, kc, :],
                                                                   start=(kc == 0), stop=(kc == 7)),
                     reads=["scT", ("Wb", wi)], writes=[("ps", b)])
            P.op("dve", lambda e, b=b, g=g: e.tensor_tensor(out=modrow[:], in0=ps[b][0:NSEG, :],
                                                            in1=adab_bc[:, :], op=ALU.add),
                 reads=[("ps", b), "adab_bc"], writes=["modrow"])
            P.dma(mod_scr[l, :, g * 512:(g + 1) * 512], modrow[:], reads=["modrow"], writes=[("mod", l)])

    def rsqrt_col(dst, src_, tag):
        P.op("act", lambda e: e.activation(out=src_, in_=src_, func=AF.Sqrt), reads=[], writes=[tag])
        P.op("dve", lambda e: e.reciprocal(out=dst, in_=src_), reads=[tag], writes=[tag])

    def seg_layer(s, l, x_src, x_dst, final, mode=0, pj=None):
        dirs = pj["dirs"] if (pj and "dirs" in pj) else (0, 1)
        emit_b = bool(pj and mode == 2 and "fin" in pj)
        xkey = pj["xkey"] if (pj and "xkey" in pj) else ("x", s)
        xdkey = pj["xdkey"] if (pj and "xdkey" in pj) else ("xd", s, l)

        def st_init(mx, dr, dst_ap, dst_key, cols=None):
            ap, key = pj["init"][(mx, dr)]
            if cols is not None:
                ap = ap[:, cols]
            P.dma(dst_ap, ap, reads=[key], writes=[dst_key])

        def st_fin(mx, dr, src_ap, src_key, cols=None):
            ap, key = pj["fin"][(mx, dr)]
            if cols is not None:
                ap = ap[:, cols]
            P.dma(ap, src_ap, reads=[src_key], writes=[key])

        P.dma(normg_bc[:], norm_g[l].partition_broadcast(128), writes=["tmpA"])
        P.dma(shift_bc[:], mod_scr[l, s, 0:D].partition_broadcast(128), reads=[("mod", l)], writes=["shift"])
        P.dma(gmod_bc[:], mod_scr[l, s, D:2 * D].partition_broadcast(128), reads=[("mod", l)], writes=["gmod"])
        P.dma(gate_bc[:], mod_scr[l, s, 2 * D:3 * D].partition_broadcast(128), reads=[("mod", l)], writes=["gate"])
        P.op("dve", lambda e: e.scalar_tensor_tensor(out=gmod_bc[:], in0=gmod_bc[:], scalar=1.0, in1=normg_bc[:],
                                                     op0=ALU.add, op1=ALU.mult),
             reads=["tmpA"], writes=["gmod"])

        for blk in range(NB):
            i = blk % 2
            c8 = blk % 8
            P.dma(xin[i][:], x_src[blk * 128:(blk + 1) * 128, :], reads=[xkey], writes=[("xin", i)])
            P.op("act", lambda e, i=i, c8=c8: e.activation(out=tmpA[:], in_=xin[i][:], func=AF.Square,
                                                           accum_out=small[:, c8:c8 + 1]),
                 reads=[("xin", i)], writes=["tmpA", ("ss", c8)])
            P.op("dve", lambda e, c8=c8: e.tensor_scalar(out=small[:, 8 + c8:9 + c8], in0=small[:, c8:c8 + 1],
                                                         scalar1=1.0 / D, scalar2=EPS, op0=ALU.mult, op1=ALU.add),
                 reads=[("ss", c8)], writes=[("ms", c8)])
            P.op("act", lambda e, c8=c8: e.activation(out=small[:, 8 + c8:9 + c8], in_=small[:, 8 + c8:9 + c8], func=AF.Sqrt),
                 reads=[], writes=[("ms", c8)])
            P.op("dve", lambda e, c8=c8: e.reciprocal(out=small[:, 16 + c8:17 + c8], in_=small[:, 8 + c8:9 + c8]),
                 reads=[("ms", c8)], writes=[("rstd", c8)])
            P.op("dve", lambda e, i=i, c8=c8: e.scalar_tensor_tensor(out=tmpA[:], in0=xin[i][:],
                                                                     scalar=small[:, 16 + c8:17 + c8],
                                                                     in1=gmod_bc[:], op0=ALU.mult, op1=ALU.mult),
                 reads=[("xin", i), ("rstd", c8), "gmod"], writes=["tmpA"])
            P.op("pool", lambda e, i=i: e.tensor_tensor(out=htm[i][:], in0=tmpA[:], in1=shift_bc[:], op=ALU.add),
                 reads=["tmpA", "shift"], writes=[("htm", 0)])
            for kc in range(8):
                P.op("pe", lambda e, i=i, kc=kc: e.transpose(psb[:, kc * 128:(kc + 1) * 128], htm[i][:, kc * 128:(kc + 1) * 128], identb[:]),
                     reads=[("htm", 0), "identb"], writes=["psb"])
            P.op("act", lambda e, blk=blk: e.copy(out=hT[:, :, blk * 128:(blk + 1) * 128],
                                                  in_=psb[:].rearrange("p (k t) -> p k t", k=8)),
                 reads=["psb"], writes=[("hT", blk // 4)])

        halo = pj.get("halo") if pj else None
        hvalid = (False, False)
        if halo is not None:
            AR.reset()
            htok = AR.get("htok", [32, D], F32)
            htmp = AR.get("htmp", [32, D], F32)
            htb = AR.get("htb", [32, D], BF16)
            hst = AR.get("hst", [32, 8], F32)
            P.op("pool", lambda e: e.memset(htok[:], 0.0), writes=["htok"])
            if halo["kind"] == "static":
                hvalid = (halo["left"] is not None, halo["right"] is not None)
                if halo["left"] is not None:
                    P.dma(htok[0:15, :], halo["left"], reads=[halo["lkey"]], writes=["htok"])
                if halo["right"] is not None:
                    P.dma(htok[16:31, :], halo["right"], reads=[halo["rkey"]], writes=["htok"])
            else:
                hvalid = (True, True)
                hg = AR.get("hg", [32, NRP, D], F32)
                ohlr = AR.get("ohlr", [32, 8], F32)
                P.op("pool", lambda e: e.memset(hg[:], 0.0), writes=["hg"])
                P.dma(ohlr[0:16, :], ohLR_in[0].partition_broadcast(16), writes=["ohlr"])
                P.dma(ohlr[16:32, :], ohLR_in[1].partition_broadcast(16), writes=["ohlr"])
                vtmp = AR.get("vtmp", [128, 16], F32)
                P.dma(vtmp[:], ohLR_in.rearrange("a b -> (a b)").partition_broadcast(128), writes=["vtmp"])
                P.op("dve", lambda e: e.reduce_sum(out=small[:, 60:61], in_=vtmp[:, 0:8], axis=AX.X), reads=["vtmp"], writes=[("vfl", 0)])
                P.op("dve", lambda e: e.reduce_sum(out=small[:, 61:62], in_=vtmp[:, 8:16], axis=AX.X), reads=["vtmp"], writes=[("vfl", 1)])
                X = halo["X"]
                P.dma(hg[0:15, :, :], X[:, T - 15:T, :].rearrange("j r d -> r j d"), reads=halo["keys"], writes=["hg"])
                P.dma(hg[16:31, :, :], X[:, 0:15, :].rearrange("j r d -> r j d"), reads=halo["keys"], writes=["hg"])
                for j in range(NRP):
                    P.op("dve", lambda e, j=j: e.scalar_tensor_tensor(out=htok[:], in0=hg[:, j, :], scalar=ohlr[:, j:j + 1], in1=htok[:],
                                                                      op0=ALU.mult, op1=ALU.add),
                         reads=["hg", "ohlr"], writes=["htok"])
            P.op("act", lambda e: e.activation(out=htmp[:], in_=htok[:], func=AF.Square, accum_out=hst[:, 0:1]), reads=["htok"], writes=["htmp", "hst"])
            P.op("dve", lambda e: e.tensor_scalar(out=hst[:, 1:2], in0=hst[:, 0:1], scalar1=1.0 / D, scalar2=EPS, op0=ALU.mult, op1=ALU.add),
                 reads=[], writes=["hst"])
            P.op("act", lambda e: e.activation(out=hst[:, 1:2], in_=hst[:, 1:2], func=AF.Sqrt), reads=[], writes=["hst"])
            P.op("dve", lambda e: e.reciprocal(out=hst[:, 2:3], in_=hst[:, 1:2]), reads=[], writes=["hst"])
            P.op("dve", lambda e: e.scalar_tensor_tensor(out=htmp[:], in0=htok[:], scalar=hst[:, 2:3], in1=gmod_bc[0:32, :], op0=ALU.mult, op1=ALU.mult),
                 reads=["htok", "hst", "gmod"], writes=["htmp"])
            P.op("dve", lambda e: e.tensor_tensor(out=htb[:], in0=htmp[:], in1=shift_bc[0:32, :], op=ALU.add), reads=["htmp", "shift"], writes=["htb"])
            for kc in range(8):
                P.op("pe", lambda e, kc=kc: e.transpose(psb[:, kc * 32:(kc + 1) * 32], htb[:, kc * 128:(kc + 1) * 128], identb[0:32, 0:32]),
                     reads=["htb", "identb"], writes=["psb"])
            P.op("act", lambda e: e.copy(out=hT32[:], in_=psb[:, 0:256].rearrange("p (k t) -> p k t", k=8)), reads=["psb"], writes=["hT32"])

        def hcopy(dst, src_, side, wkey, b):
            if halo["kind"] == "sel":
                P.op("act", lambda e: e.activation(out=dst, in_=src_, func=AF.Copy, scale=small[:, 60 + side:61 + side]),
                     reads=[("ps", b), ("vfl", side)], writes=[wkey])
            else:
                P.op("act", lambda e: e.copy(out=dst, in_=src_), reads=[("ps", b)], writes=[wkey])

        def inproj_fm(col0, evac, ncols=512, halo_evac=None):
            wi = load_w(wbf_in[l], col0, ncols, 0, 8, ("wbf_in", l))
            for m in range(ncols // 128):
                for tt in range(NT):
                    b = next_ps()
                    for kc in range(8):
                        P.op("pe", lambda e, kc=kc, wi=wi, b=b, m=m, tt=tt: e.matmul(
                            ps[b][:, :], lhsT=Wb[wi][:, kc, m * 128:(m + 1) * 128], rhs=hT[:, kc, tt * 512:(tt + 1) * 512],
                            start=(kc == 0), stop=(kc == 7)),
                            reads=[("Wb", wi), ("hT", tt)], writes=[("ps", b)])
                    evac(m, tt, b)
                if halo_evac is not None and halo is not None:
                    b = next_ps()
                    for kc in range(8):
                        P.op("pe", lambda e, kc=kc, wi=wi, b=b, m=m: e.matmul(
                            ps[b][:, 0:32], lhsT=Wb[wi][:, kc, m * 128:(m + 1) * 128], rhs=hT32[:, kc, :],
                            start=(kc == 0), stop=(kc == 7)),
                            reads=[("Wb", wi), "hT32"], writes=[("ps", b)])
                    halo_evac(m, b)

        def inproj_tm(col0, ncols, evac):
            wi = load_w(wbf_in[l], col0, ncols, 0, 8, ("wbf_in", l))
            for blk in range(NB):
                b = next_ps()
                for kc in range(8):
                    P.op("pe", lambda e, kc=kc, wi=wi, b=b, blk=blk: e.matmul(
                        ps[b][:, 0:ncols], lhsT=hT[:, kc, blk * 128:(blk + 1) * 128], rhs=Wb[wi][:, kc, 0:ncols],
                        start=(kc == 0), stop=(kc == 7)),
                        reads=[("Wb", wi), ("hT", blk // 4)], writes=[("ps", b)])
                evac(blk, b)

        def z_evac(m, tt, b):
            P.op("act", lambda e: e.activation(out=mixedT[:, m, tt * 512:(tt + 1) * 512], in_=ps[b][:, :], func=AF.Silu),
                 reads=[("ps", b)], writes=[("mix", m, tt)])

        def outproj(mi, first, last):
            wi = rr["w"] % 2
            rr["w"] += 1
            for g in range(2):
                P.dma(Wb[wi][:, 4 * g:4 * g + 4, :],
                      wbf_out[l][mi * 512:(mi + 1) * 512, g * 512:(g + 1) * 512].rearrange("(kc p) n -> p kc n", p=128),
                      reads=[("wbf_out", l)], writes=[("Wb", wi)])
            src_ap, src_key = (x_src, xkey) if first else (x_acc[s], ("xacc", s))
            dst_ap, dst_key = (x_dst, xdkey) if last else (x_acc[s], ("xacc", s))
            for blk in range(NB):
                i = blk % 2
                P.dma(xin[i][:], src_ap[blk * 128:(blk + 1) * 128, :], reads=[src_key], writes=[("xin", i)])
                for g in range(2):
                    b = next_ps()
                    gs = slice(g * 512, (g + 1) * 512)
                    for kc in range(4):
                        P.op("pe", lambda e, kc=kc, b=b, blk=blk, g=g: e.matmul(
                            ps[b][:, :], lhsT=mixedT[:, kc, blk * 128:(blk + 1) * 128], rhs=Wb[wi][:, 4 * g + kc, :],
                            start=(kc == 0), stop=(kc == 3)),
                            reads=[("Wb", wi), ("mix", kc, blk // 4)], writes=[("ps", b)])
                    P.op("dve", lambda e, b=b, gs=gs, i=i: e.tensor_tensor(out=xo[i][:, gs], in0=ps[b][:, :], in1=gate_bc[:, gs], op=ALU.mult),
                         reads=[("ps", b), "gate"], writes=[("xo", 0)])
                    P.op("pool", lambda e, gs=gs, i=i: e.tensor_tensor(out=xo[i][:, gs], in0=xo[i][:, gs], in1=xin[i][:, gs], op=ALU.add),
                         reads=[("xin", i)], writes=[("xo", 0)])
                if final and last:
                    c = 24 + blk % 8
                    c8 = blk % 8
                    P.op("act", lambda e, i=i, c=c: e.activation(out=tmpA[:], in_=xo[i][:], func=AF.Square, accum_out=small[:, c:c + 1]),
                         reads=[("xo", 0)], writes=["tmpA", ("fss", c8)])
                    P.op("dve", lambda e, c=c: e.tensor_scalar(out=small[:, c + 8:c + 9], in0=small[:, c:c + 1], scalar1=1.0 / D, scalar2=EPS,
                                                               op0=ALU.mult, op1=ALU.add),
                         reads=[("fss", c8)], writes=[("fms", c8)])
                    P.op("act", lambda e, c=c: e.activation(out=small[:, c + 8:c + 9], in_=small[:, c + 8:c + 9], func=AF.Sqrt),
                         reads=[], writes=[("fms", c8)])
                    P.op("dve", lambda e, c=c: e.reciprocal(out=small[:, c + 16:c + 17], in_=small[:, c + 8:c + 9]),
                         reads=[("fms", c8)], writes=[("frs", c8)])
                    P.op("dve", lambda e, i=i, c=c: e.scalar_tensor_tensor(out=xo[i][:], in0=xo[i][:], scalar=small[:, c + 16:c + 17],
                                                                           in1=finalg_bc[:], op0=ALU.mult, op1=ALU.mult),
                         reads=[("frs", c8), "finalg"], writes=[("xo", 0)])
                P.dma(dst_ap[blk * 128:(blk + 1) * 128, :], xo[i][:], reads=[("xo", 0)], writes=[dst_key])

        def mixer_A():
            AR.reset()
            u_buf = AR.get("u_buf", [128, 4, T + 2 * HALO], F32, nsub=4)
            accA = AR.get("accA", [128, 4, T], F32, nsub=4)
            convw = AR.get("convw", [128, 4, KA], F32)
            convb = AR.get("convb", [128, 4], F32)
            lng = AR.get("lng", [128, 4], F32)
            lnb = AR.get("lnb", [128, 4], F32)
            sig = [AR.get("sig%d" % i, [128, 512], F32) for i in range(2)]
            lnbf = AR.get("lnbf", [128, 4, 512], BF16)
            lnsq = AR.get("lnsq", [128, 4, 512], BF16)
            lnt = [AR.get("lnt%d" % i, [128, 512], F32) for i in range(4)]
            P.dma(convw[:], conv_a_wT[l].rearrange("(cc p) k -> p cc k", p=128), writes=["convw"])
            P.dma(convb[:], conv_a_b[l].rearrange("(cc p) -> p cc", p=128), writes=["convb"], allow_slow_non_contiguous=True)
            P.dma(lng[:], ln_a_g[l].rearrange("(cc p) -> p cc", p=128), writes=["lng"], allow_slow_non_contiguous=True)
            P.dma(lnb[:], ln_a_b[l].rearrange("(cc p) -> p cc", p=128), writes=["lnb"], allow_slow_non_contiguous=True)

            def a_val(m, tt, b):
                P.op("act", lambda e: e.copy(out=u_buf[:, m, HALO + tt * 512:HALO + (tt + 1) * 512], in_=ps[b][:, :]),
                     reads=[("ps", b)], writes=[("u_buf", m)])

            def a_glu(m, tt, b):
                i = (m * NT + tt) % 2
                P.op("act", lambda e: e.activation(out=sig[i][:], in_=ps[b][:, :], func=AF.Sigmoid),
                     reads=[("ps", b)], writes=["sig%d" % i])
                P.op("pool", lambda e: e.tensor_tensor(out=u_buf[:, m, HALO + tt * 512:HALO + (tt + 1) * 512],
                                                       in0=u_buf[:, m, HALO + tt * 512:HALO + (tt + 1) * 512], in1=sig[i][:], op=ALU.mult),
                     reads=["sig%d" % i], writes=[("u_buf", m)])

            for m in range(4):
                P.op("pool", lambda e, m=m: e.memset(u_buf[:, m, 0:HALO], 0.0), writes=[("u_buf", m)])
                P.op("pool", lambda e, m=m: e.memset(u_buf[:, m, HALO + T:], 0.0), writes=[("u_buf", m)])
            hsig = AR.get("hsig", [128, 32], F32)

            def a_val_h(m, b):
                if hvalid[0]:
                    hcopy(u_buf[:, m, 0:HALO], ps[b][:, 0:15], 0, ("u_buf", m), b)
                if hvalid[1]:
                    hcopy(u_buf[:, m, HALO + T:HALO + T + 15], ps[b][:, 16:31], 1, ("u_buf", m), b)

            def a_glu_h(m, b):
                P.op("act", lambda e: e.activation(out=hsig[:], in_=ps[b][:, 0:32], func=AF.Sigmoid), reads=[("ps", b)], writes=["hsig"])
                if hvalid[0]:
                    P.op("dve", lambda e: e.tensor_tensor(out=u_buf[:, m, 0:HALO], in0=u_buf[:, m, 0:HALO], in1=hsig[:, 0:15], op=ALU.mult),
                         reads=["hsig"], writes=[("u_buf", m)])
                if hvalid[1]:
                    P.op("dve", lambda e: e.tensor_tensor(out=u_buf[:, m, HALO + T:HALO + T + 15], in0=u_buf[:, m, HALO + T:HALO + T + 15],
                                                          in1=hsig[:, 16:31], op=ALU.mult),
                         reads=["hsig"], writes=[("u_buf", m)])
            inproj_fm(2 * WG, z_evac)
            inproj_fm(0, a_val, halo_evac=a_val_h)
            inproj_fm(WG, a_glu, halo_evac=a_glu_h)
            for cc in range(4):
                P.op("dve", lambda e, cc=cc: e.tensor_scalar(out=accA[:, cc, :], in0=u_buf[:, cc, 0:T], scalar1=convw[:, cc, 0:1],
                                                             scalar2=convb[:, cc:cc + 1], op0=ALU.mult, op1=ALU.add),
                     reads=[("u_buf", cc), "convw", "convb"], writes=[("accA", cc)])
                for k in range(1, KA):
                    P.op("dve", lambda e, cc=cc, k=k: e.scalar_tensor_tensor(out=accA[:, cc, :], in0=u_buf[:, cc, k:k + T],
                                                                             scalar=convw[:, cc, k:k + 1], in1=accA[:, cc, :],
                                                                             op0=ALU.mult, op1=ALU.add),
                         reads=[("u_buf", cc), "convw"], writes=[("accA", cc)])
            for tt in range(NT):
                sl = slice(tt * 512, (tt + 1) * 512)
                P.op("pool", lambda e, sl=sl: e.tensor_copy(out=lnbf[:], in_=accA[:, :, sl]),
                     reads=["accA"], writes=["lnbf"])
                P.op("act", lambda e, sl=sl: e.activation(out=lnsq[:], in_=accA[:, :, sl], func=AF.Square),
                     reads=["accA"], writes=["lnsq"])
                bm, be = next_ps(), next_ps()
                for cc in range(4):
                    P.op("pe", lambda e, cc=cc, bm=bm: e.matmul(ps[bm][:, :], lhsT=onesb[:], rhs=lnbf[:, cc, :],
                                                                start=(cc == 0), stop=(cc == 3)),
                         reads=["onesb", "lnbf"], writes=[("ps", bm)])
                for cc in range(4):
                    P.op("pe", lambda e, cc=cc, be=be: e.matmul(ps[be][:, :], lhsT=onesb[:], rhs=lnsq[:, cc, :],
                                                                start=(cc == 0), stop=(cc == 3)),
                         reads=["onesb", "lnsq"], writes=[("ps", be)])
                P.op("act", lambda e, bm=bm: e.activation(out=lnt[0][:], in_=ps[bm][:, :], func=AF.Square),
                     reads=[("ps", bm)], writes=["lnt0"])
                P.op("act", lambda e, bm=bm: e.copy(out=lnt[2][:], in_=ps[bm][:, :]), reads=[("ps", bm)], writes=["lnt2"])
                P.op("dve", lambda e, be=be: e.tensor_tensor(out=lnt[1][:], in0=ps[be][:, :], in1=lnt[0][:], op=ALU.subtract),
                     reads=[("ps", be), "lnt0"], writes=["lnt1"])
                P.op("dve", lambda e: e.tensor_scalar_add(out=lnt[1][:], in0=lnt[1][:], scalar1=EPS), reads=[], writes=["lnt1"])
                P.op("act", lambda e: e.activation(out=lnt[1][:], in_=lnt[1][:], func=AF.Sqrt), reads=[], writes=["lnt1"])
                P.op("dve", lambda e: e.reciprocal(out=lnt[1][:], in_=lnt[1][:]), reads=[], writes=["lnt1"])
                for cc in range(4):
                    P.op("pool", lambda e, cc=cc, sl=sl: e.tensor_tensor(out=lnt[3][:], in0=accA[:, cc, sl], in1=lnt[2][:], op=ALU.subtract),
                         reads=[("accA", cc), "lnt2"], writes=["lnt3"])
                    P.op("pool", lambda e: e.tensor_tensor(out=lnt[3][:], in0=lnt[3][:], in1=lnt[1][:], op=ALU.mult),
                         reads=["lnt1"], writes=["lnt3"])
                    P.op("act", lambda e, cc=cc: e.activation(out=lnt[3][:], in_=lnt[3][:], func=AF.Silu,
                                                              scale=lng[:, cc:cc + 1], bias=lnb[:, cc:cc + 1]),
                         reads=["lng", "lnb"], writes=["lnt3"])
                    P.op("pool", lambda e, cc=cc, sl=sl: e.tensor_tensor(out=mixedT[:, cc, sl], in0=mixedT[:, cc, sl], in1=lnt[3][:], op=ALU.mult),
                         reads=["lnt3"], writes=[("mix", cc, tt)])

        def mixer_B():
            CB = 3 * WG
            AR.reset()
            cosT = AR.get("cosT", [128, T], F32)
            sinT = AR.get("sinT", [128, T], F32)
            qT = AR.get("qT", [128, 4, T], BF16, nsub=4)
            kT = AR.get("kT", [128, 4, T], BF16, nsub=4)
            vB = AR.get("vB", [128, NB, 512], BF16)
            rc = AR.get("rc", [128, 7, 512], F32)
            retg = AR.get("retg", [128, 4], F32)
            t1 = [AR.get("t1_0", [128, 512], F32)] * 2
            t2 = [AR.get("t2_0", [128, 512], F32)] * 2
            kd = [AR.get("kd%d" % i, [128, 512], BF16) for i in range(2)]
            PT = [AR.get("PT%d" % i, [128, 512], BF16) for i in range(2)]
            qdf = [AR.get("qdf%d" % i, [128, 512], BF16) for i in range(2)]
            qdb = [AR.get("qdb%d" % i, [128, 512], BF16) for i in range(2)]
            Sf = AR.get("Sf", [128, 512], F32)
            Sb = AR.get("Sb", [128, 512], F32)
            Sfb = [AR.get("Sfb%d" % i, [128, 512], BF16) for i in range(2)]
            sqb = AR.get("sqb", [128, 512], BF16)
            rr_ = AR.get("rr_", [128, 512], F32)
            tn = AR.get("tn", [128, 512], F32)
            SbS = AR.get("SbS", [128, NB, 512], BF16, alias="cosT")
            rope_ap = pj["rope"] if (pj and "rope" in pj) else rope[rope_idx[s]]
            P.dma(cosT[:], rope_ap[0], writes=["cosT"])
            P.dma(sinT[:], rope_ap[1], writes=["sinT"])
            P.dma(rc[:], retc[:, :, :], writes=["rc"])
            P.dma(retg[:], ret_norm_g[l].rearrange("(h p) -> p h", p=128), writes=["retg"], allow_slow_non_contiguous=True)
            if mode != 1:
                inproj_fm(CB + 3 * WG, z_evac)

            cnt = [0]
            for which, dstT, dname in (((0, qT, "qT"), (1, kT, "kT")) if mode != 1 else ((1, kT, "kT"),)):
                for pair in range(2):
                    wi = rr["w"] % 2
                    rr["w"] += 1
                    for hh in range(2):
                        h = 2 * pair + hh
                        cb = CB + which * WG + h * 128
                        srcw = wbf_in[l]
                        for (d0, s0, n) in ((2 * hh * 128, cb, 128), ((2 * hh + 1) * 128, cb + 64, 64), ((2 * hh + 1) * 128 + 64, cb, 64)):
                            P.dma(Wb[wi][:, :, d0:d0 + n], srcw[:, s0:s0 + n].rearrange("(kc p) n -> p kc n", p=128),
                                  reads=[("wbf_in", l)], writes=[("Wb", wi)])
                    for hh in range(2):
                        h = 2 * pair + hh
                        for tt in range(NT):
                            bq, bs = next_ps(), next_ps()
                            for (b, mt) in ((bq, 2 * hh), (bs, 2 * hh + 1)):
                                for kc in range(8):
                                    P.op("pe", lambda e, kc=kc, wi=wi, b=b, mt=mt, tt=tt: e.matmul(
                                        ps[b][:, :], lhsT=Wb[wi][:, kc, mt * 128:(mt + 1) * 128], rhs=hT[:, kc, tt * 512:(tt + 1) * 512],
                                        start=(kc == 0), stop=(kc == 7)),
                                        reads=[("Wb", wi), ("hT", tt)], writes=[("ps", b)])
                            i = cnt[0] % 2
                            cnt[0] += 1
                            sl = slice(tt * 512, (tt + 1) * 512)
                            P.op("dve", lambda e, i=i, bq=bq, sl=sl: e.tensor_tensor(out=t1[i][:], in0=ps[bq][:, :], in1=cosT[:, sl], op=ALU.mult),
                                 reads=[("ps", bq), "cosT"], writes=["t1_0"])
                            P.op("dve", lambda e, i=i, bs=bs, sl=sl: e.tensor_tensor(out=t2[i][:], in0=ps[bs][:, :], in1=sinT[:, sl], op=ALU.mult),
                                 reads=[("ps", bs), "sinT"], writes=["t2_0"])
                            P.op("pool", lambda e, i=i, h=h, sl=sl, dstT=dstT: e.tensor_tensor(out=dstT[:, h, sl], in0=t1[i][:], in1=t2[i][:], op=ALU.add),
                                 reads=["t1_0", "t2_0"], writes=[(dname, h)])

            def v_evac(blk, b):
                P.op("act", lambda e: e.copy(out=vB[:, blk, :], in_=ps[b][:, :]), reads=[("ps", b)], writes=["vB"])
            inproj_tm(CB + 2 * WG, 512, v_evac)

            def kdec(n, tab, i):
                for h in range(4):
                    P.op("pe", lambda e, h=h: e.transpose(psb[:, h * 128:(h + 1) * 128], kT[:, h, n * 128:(n + 1) * 128], identb[:]),
                         reads=[("kT", h), "identb"], writes=["psb"])
                P.op("dve", lambda e: e.tensor_tensor(out=kd[i][:], in0=psb[:, 0:512], in1=rc[:, tab, :], op=ALU.mult),
                     reads=["psb", "rc"], writes=["kd%d" % i])

            def kv_update(n, i, S, Sname, ctab):
                b = next_ps()
                for h in range(4):
                    hs = slice(h * 128, (h + 1) * 128)
                    P.op("pe", lambda e, hs=hs: e.matmul(ps[b][:, hs], lhsT=kd[i][:, hs], rhs=vB[:, n, hs], start=True, stop=True),
                         reads=["kd%d" % i, "vB"], writes=[("ps", b)])
                P.op("pool", lambda e: e.tensor_tensor(out=S[:], in0=S[:], in1=rc[:, ctab, :], op=ALU.mult),
                     reads=["rc"], writes=[Sname])
                P.op("dve", lambda e: e.tensor_tensor(out=S[:], in0=S[:], in1=ps[b][:, :], op=ALU.add),
                     reads=[("ps", b)], writes=[Sname])

            if mode != 0:
                if 1 in dirs:
                    st_init("B", 1, Sb[:], "Sb")
                if 0 in dirs:
                    st_init("B", 0, Sf[:], "Sf")
            else:
                P.op("pool", lambda e: e.memset(Sb[:], 0.0), writes=["Sb"])
                P.op("pool", lambda e: e.memset(Sf[:], 0.0), writes=["Sf"])
            if 1 in dirs:
                for n in range(NB - 1, -1, -1):
                    i = n % 2
                    P.op("act", lambda e, n=n: e.copy(out=SbS[:, n, :], in_=Sb[:]), reads=["Sb"], writes=["SbS"])
                    if n > 0 or mode == 1 or emit_b:
                        kdec(n, 4, i)
                        kv_update(n, i, Sb, "Sb", 6)
                if mode == 1 or emit_b:
                    st_fin("B", 1, Sb[:], "Sb")
            def fwd_block(n):
                i = n % 2
                bsl = slice(n * 128, (n + 1) * 128)
                kdec(n, 3, i)
                if mode == 1:
                    kv_update(n, i, Sf, "Sf", 5)
                    return
                bsc = next_ps()
                for h in range(4):
                    hs = slice(h * 128, (h + 1) * 128)
                    P.op("pe", lambda e, h=h, hs=hs: e.matmul(ps[bsc][:, hs], lhsT=kT[:, h, bsl], rhs=qT[:, h, bsl], start=True, stop=True),
                         reads=[("kT", h), ("qT", h)], writes=[("ps", bsc)])
                P.op("dve", lambda e: e.tensor_tensor(out=PT[i][:], in0=ps[bsc][:, :], in1=rc[:, 0, :], op=ALU.mult),
                     reads=[("ps", bsc), "rc"], writes=["PT%d" % i])
                P.op("pool", lambda e: e.tensor_tensor(out=qdf[i][:].rearrange("p (h t) -> p h t", h=4), in0=qT[:, :, bsl],
                                                       in1=rc[:, 1, :].rearrange("p (h t) -> p h t", h=4), op=ALU.mult),
                     reads=["qT", "rc"], writes=["qdf%d" % i])
                P.op("pool", lambda e: e.tensor_tensor(out=qdb[i][:].rearrange("p (h t) -> p h t", h=4), in0=qT[:, :, bsl],
                                                       in1=rc[:, 2, :].rearrange("p (h t) -> p h t", h=4), op=ALU.mult),
                     reads=["qT", "rc"], writes=["qdb%d" % i])
                P.op("act", lambda e: e.copy(out=Sfb[i][:], in_=Sf[:]), reads=["Sf"], writes=["Sfb%d" % i])
                bo = next_ps()
                for h in range(4):
                    hs = slice(h * 128, (h + 1) * 128)
                    P.op("pe", lambda e, hs=hs: e.matmul(ps[bo][:, hs], lhsT=vB[:, n, hs], rhs=PT[i][:, hs], start=True, stop=False),
                         reads=["vB", "PT%d" % i], writes=[("ps", bo)])
                    P.op("pe", lambda e, hs=hs: e.matmul(ps[bo][:, hs], lhsT=Sfb[i][:, hs], rhs=qdf[i][:, hs], start=False, stop=False),
                         reads=["Sfb%d" % i, "qdf%d" % i], writes=[("ps", bo)])
                    P.op("pe", lambda e, hs=hs: e.matmul(ps[bo][:, hs], lhsT=SbS[:, n, hs], rhs=qdb[i][:, hs], start=False, stop=True),
                         reads=["SbS", "qdb%d" % i], writes=[("ps", bo)])
                P.op("act", lambda e: e.activation(out=sqb[:], in_=ps[bo][:, :], func=AF.Square), reads=[("ps", bo)], writes=["sqb"])
                bm = next_ps()
                P.op("pe", lambda e: e.matmul(ps[bm][:, :], lhsT=onesh[:], rhs=sqb[:], start=True, stop=True),
                     reads=["onesh", "sqb"], writes=[("ps", bm)])
                P.op("dve", lambda e: e.tensor_scalar_add(out=rr_[:], in0=ps[bm][:, :], scalar1=EPS), reads=[("ps", bm)], writes=["rr_"])
                P.op("act", lambda e: e.activation(out=rr_[:], in_=rr_[:], func=AF.Sqrt), reads=[], writes=["rr_"])
                P.op("dve", lambda e: e.reciprocal(out=rr_[:], in_=rr_[:]), reads=[], writes=["rr_"])
                P.op("dve", lambda e: e.tensor_tensor(out=tn[:], in0=ps[bo][:, :], in1=rr_[:], op=ALU.mult),
                     reads=[("ps", bo), "rr_"], writes=["tn"])
                for h in range(4):
                    hs = slice(h * 128, (h + 1) * 128)
                    P.op("dve", lambda e, h=h, hs=hs: e.scalar_tensor_tensor(out=mixedT[:, h, bsl], in0=tn[:, hs], scalar=retg[:, h:h + 1],
                                                                             in1=mixedT[:, h, bsl], op0=ALU.mult, op1=ALU.mult),
                         reads=["tn", "retg"], writes=[("mix", h, n // 4)])
                if n < NB - 1:
                    kv_update(n, i, Sf, "Sf", 5)

            if 0 in dirs:
                for n in range(NB):
                    fwd_block(n)
                if mode == 1:
                    st_fin("B", 0, Sf[:], "Sf")


        def mixer_C():
            CC = 7 * WG
            LCL = 41.0
            SC = 128.0 ** -0.5
            AR.reset()
            lbt = AR.get("lbt", [128, 2, 4], F32)
            omlt = AR.get("omlt", [128, 2, 4], F32)
            lx0 = AR.get("lx0", [128, 2, 4], F32)
            lx1 = AR.get("lx1", [128, 2, 4], F32)
            hgg = AR.get("hgg", [128, 4], F32)
            gm = AR.get("gm", [128, 2, 128], F32)
            one1 = AR.get("one1", [128, 1], F32)
            onesC = AR.get("onesC", [128, 128], BF16)
            qs = AR.get("qs", [128, 2, T], BF16, nsub=2)
            vC = AR.get("vC", [128, NB, 256], BF16)
            Ab = AR.get("Ab", [128, 2, T + 2], F32, nsub=2)
            kb = AR.get("kb", [128, 2, T], BF16, nsub=2)
            oacc = AR.get("oacc", [128, 2, T], F32)
            Gt = AR.get("Gt", [128, T], F32)
            sg = [AR.get("sg0", [128, 512], F32)] * 2
            ft = [AR.get("ft0", [128, 512], F32)] * 2
            E0 = AR.get("E0", [128, 256], F32)
            E1 = AR.get("E1", [128, 256], F32)
            E2 = AR.get("E2", [128, 256], F32)
            E3 = AR.get("E3", [128, 256], F32)
            qt = [AR.get("qt%d" % i, [128, 256], BF16) for i in range(2)]
            kt = [AR.get("kt%d" % i, [128, 256], BF16) for i in range(2)]
            qi = [AR.get("qi%d" % i, [128, 256], BF16) for i in range(2)]
            Ek = AR.get("Ek", [128, 4, 2, 128], F32)
            kstT = [AR.get("kstT%d" % i, [128, 4, 2, 128], BF16) for i in range(2)]
            kstm = [AR.get("kstm0", [128, 1024], BF16)] * 2
            Sbs = AR.get("Sbs", [128, 256], F32)
            Sall = AR.get("Sall", [128, 4, 256], F32)
            Sallb = [AR.get("Sallb%d" % i, [128, 4, 256], BF16) for i in range(2)]
            dec = AR.get("dec", [128, 4, 2], F32)
            PTc = [AR.get("PTc%d" % i, [128, 256], BF16) for i in range(2)]
            sqc = AR.get("sqc", [128, 256], BF16)
            rc_ = AR.get("rc_", [128, 256], F32)
            tnc = AR.get("tnc", [128, 256], F32)
            osum = AR.get("osum", [128, 256], F32)
            gtt = AR.get("gtt", [128, 2], F32)

            P.op("pool", lambda e: e.memset(one1[:], 1.0), writes=["one1"])
            P.op("pool", lambda e: e.memset(onesC[:], SC * SC / 128.0), writes=["onesC"])
            P.dma(gm[:], gmask[:, :, :], writes=["gm"])
            P.dma(hgg[:], hgrn_norm_g[l].rearrange("(h p) -> p h", p=128), writes=["hgg"], allow_slow_non_contiguous=True)
            if l == 0:
                P.op("pool", lambda e: e.memset(lbt[:], 0.0), writes=["lbt"])
                P.op("pool", lambda e: e.memset(omlt[:], 1.0), writes=["omlt"])
            else:
                P.dma(lx0[:], hgrn_lb_logits[0].rearrange("r (h p) -> p r h", p=128), writes=["lx0"], allow_slow_non_contiguous=True)
                P.dma(lx1[:], hgrn_lb_logits[1].rearrange("r (h p) -> p r h", p=128), writes=["lx1"], allow_slow_non_contiguous=True)
                P.op("dve", lambda e: e.tensor_tensor(out=lx1[:], in0=lx1[:], in1=lx0[:], op=ALU.subtract), reads=["lx0"], writes=["lx1"])
                P.op("act", lambda e: e.activation(out=lbt[:], in_=lx1[:], func=AF.Sigmoid), reads=["lx1"], writes=["lbt"])
                P.op("act", lambda e: e.activation(out=omlt[:], in_=lx1[:], func=AF.Sigmoid, scale=-1.0), reads=["lx1"], writes=["omlt"])
            if mode != 1:
                inproj_fm(CC + 4 * WG, z_evac)

            def half_pass(half):
                h0 = 2 * half

                def q_evac(m, tt, b):
                    P.op("act", lambda e: e.activation(out=qs[:, m, tt * 512:(tt + 1) * 512], in_=ps[b][:, :], func=AF.Silu),
                         reads=[("ps", b)], writes=[("qs", m)])
                if mode != 1:
                    inproj_fm(CC + h0 * 128, q_evac, ncols=256)

                def v_evac(blk, b):
                    P.op("act", lambda e: e.copy(out=vC[:, blk, :], in_=ps[b][:, 0:256]), reads=[("ps", b)], writes=["vC"])
                inproj_tm(CC + 3 * WG + h0 * 128, 256, v_evac)

                def dir_pass(dr):
                    cnt = [0]

                    def f_evac(m, tt, b):
                        i = cnt[0] % 2
                        cnt[0] += 1
                        h = h0 + m
                        sl = slice(1 + tt * 512, 1 + (tt + 1) * 512)
                        P.op("act", lambda e: e.activation(out=sg[i][:], in_=ps[b][:, :], func=AF.Sigmoid),
                             reads=[("ps", b)], writes=["sg0"])
                        P.op("dve", lambda e: e.tensor_scalar(out=ft[i][:], in0=sg[i][:], scalar1=omlt[:, dr, h:h + 1], scalar2=lbt[:, dr, h:h + 1],
                                                              op0=ALU.mult, op1=ALU.add),
                             reads=["sg0", "omlt", "lbt"], writes=["ft0"])
                        P.op("act", lambda e: e.activation(out=Ab[:, m, sl], in_=ft[i][:], func=AF.Ln),
                             reads=["ft0"], writes=[("Ab", m)])
                        P.op("pool", lambda e: e.tensor_scalar(out=kb[:, m, tt * 512:(tt + 1) * 512], in0=ft[i][:], scalar1=-1.0, scalar2=1.0,
                                                               op0=ALU.mult, op1=ALU.add),
                             reads=["ft0"], writes=[("kb", m)])
                    inproj_fm(CC + (1 + dr) * WG + h0 * 128, f_evac, ncols=256)
                    for m in range(2):
                        if dr == 0:
                            P.op("dve", lambda e, m=m: e.tensor_tensor_scan(out=Ab[:, m, 1:T + 1], data0=one1[:, 0:1].broadcast_to([128, T]),
                                                                            data1=Ab[:, m, 1:T + 1], initial=0.0, op0=ALU.mult, op1=ALU.add),
                                 reads=["one1"], writes=[("Ab", m)])
                            P.op("pool", lambda e, m=m: e.memset(Ab[:, m, 0:1], 0.0), writes=[("Ab", m)])
                        else:
                            P.op("dve", lambda e, m=m: e.tensor_tensor_scan(out=Gt[:], data0=one1[:, 0:1].broadcast_to([128, T]),
                                                                            data1=Ab[:, m, 1:T + 1], initial=0.0, op0=ALU.mult, op1=ALU.add),
                                 reads=["one1", ("Ab", m)], writes=["Gt"])
                            P.op("dve", lambda e, m=m: e.tensor_tensor(out=Ab[:, m, 1:T + 1], in0=Ab[:, m, 1:T + 1], in1=Gt[:], op=ALU.subtract),
                                 reads=["Gt"], writes=[("Ab", m)])
                            P.op("dve", lambda e, m=m: e.tensor_scalar(out=Ab[:, m, T + 1:T + 2], in0=Gt[:, T - 1:T], scalar1=-1.0, scalar2=None,
                                                                       op0=ALU.mult),
                                 reads=["Gt"], writes=[("Ab", m)])
                    if mode != 0:
                        st_init("C", dr, Sbs[:], "Sbs", cols=slice(half * 256, (half + 1) * 256))
                    else:
                        P.op("pool", lambda e: e.memset(Sbs[:], 0.0), writes=["Sbs"])

                    def block(n):
                        i = n % 2
                        c0 = n * 128
                        bsl = slice(c0, c0 + 128)
                        Ablk = Ab[:, :, 1 + c0:1 + c0 + 128]
                        A4 = Ablk.rearrange("p h (c t) -> p h c t", c=4)
                        midc = 15 if dr == 0 else 16
                        mid = A4[:, :, :, midc:midc + 1].broadcast_to([128, 2, 4, 32])
                        if dr == 0:
                            rin = Ab[:, :, c0:c0 + 128].rearrange("p h (c t) -> p h c t", c=4)[:, :, :, 0:1].broadcast_to([128, 2, 4, 32])
                        else:
                            rin = Ab[:, :, c0 + 33:c0 + 161].rearrange("p h (c t) -> p h c t", c=4)[:, :, :, 0:1].broadcast_to([128, 2, 4, 32]) \
                                if c0 + 161 <= T + 2 else None
                        v4 = lambda ap: ap[:].rearrange("p (h c t) -> p h c t", h=2, c=4)
                        v3 = lambda ap: ap[:].rearrange("p (h t) -> p h t", h=2)
                        if mode != 1:
                            P.op("dve", lambda e: e.tensor_tensor(out=v4(E0), in0=A4, in1=mid, op=ALU.subtract), reads=["Ab"], writes=["E0"])
                            P.op("dve", lambda e: e.tensor_scalar_min(out=E1[:], in0=E0[:], scalar1=LCL), reads=["E0"], writes=["E1"])
                            P.op("act", lambda e: e.activation(out=E1[:], in_=E1[:], func=AF.Exp), reads=[], writes=["E1"])
                            P.op("pool", lambda e: e.tensor_tensor(out=v3(qt[i]), in0=qs[:, :, bsl], in1=v3(E1), op=ALU.mult),
                                 reads=["qs", "E1"], writes=["qt%d" % i])
                            P.op("dve", lambda e: e.tensor_scalar(out=E2[:], in0=E0[:], scalar1=-1.0, scalar2=LCL, op0=ALU.mult, op1=ALU.min),
                                 reads=["E0"], writes=["E2"])
                            P.op("act", lambda e: e.activation(out=E2[:], in_=E2[:], func=AF.Exp), reads=[], writes=["E2"])
                            P.op("pool", lambda e: e.tensor_tensor(out=v3(kt[i]), in0=kb[:, :, bsl], in1=v3(E2), op=ALU.mult),
                                 reads=["kb", "E2"], writes=["kt%d" % i])
                            if dr == 0:
                                P.op("dve", lambda e: e.tensor_tensor(out=v4(E3), in0=A4, in1=rin, op=ALU.subtract), reads=["Ab"], writes=["E3"])
                            else:
                                r2 = Ab[:, :, c0 + 33:c0 + 33 + 128] if c0 + 161 <= T + 2 else None
                                rinb = Ab[:, :, c0 + 33:c0 + 33 + 97].rearrange("p h (c t) -> p h c t", t=1) if False else None
                                for c in range(4):
                                    P.op("dve", lambda e, c=c: e.tensor_tensor(
                                        out=v4(E3)[:, :, c, :], in0=A4[:, :, c, :],
                                        in1=Ab[:, :, 1 + c0 + 32 * (c + 1):2 + c0 + 32 * (c + 1)].broadcast_to([128, 2, 32]), op=ALU.subtract),
                                        reads=["Ab"], writes=["E3"])
                            P.op("act", lambda e: e.activation(out=E3[:], in_=E3[:], func=AF.Exp), reads=[], writes=["E3"])
                            P.op("pool", lambda e: e.tensor_tensor(out=v3(qi[i]), in0=qs[:, :, bsl], in1=v3(E3), op=ALU.mult),
                                 reads=["qs", "E3"], writes=["qi%d" % i])
                        P.op("pool", lambda e: e.memset(kstT[i][:], 0.0), writes=["kstT%d" % i])
                        for v in range(4):
                            if dr == 0:
                                hi = 32 * (v + 1)
                                cs = slice(0, hi)
                                ref = Ablk[:, :, hi - 1:hi]
                                prev = Ab[:, :, c0:c0 + 1]
                            else:
                                lo = 32 * v
                                cs = slice(lo, 128)
                                ref = Ablk[:, :, lo:lo + 1]
                                prev = Ab[:, :, 1 + c0 + 128:2 + c0 + 128]
                            w = cs.stop - cs.start
                            P.op("dve", lambda e, v=v, cs=cs, ref=ref, w=w: e.tensor_tensor(out=Ek[:, v, :, cs], in0=ref.broadcast_to([128, 2, w]),
                                                                                          in1=Ablk[:, :, cs], op=ALU.subtract),
                                 reads=["Ab"], writes=["Ek"])
                            P.op("act", lambda e, v=v, cs=cs: e.activation(out=Ek[:, v, :, cs], in_=Ek[:, v, :, cs], func=AF.Exp), reads=[], writes=["Ek"])
                            P.op("pool", lambda e, v=v, cs=cs: e.tensor_tensor(out=kstT[i][:, v, :, cs], in0=kb[:, :, c0 + cs.start:c0 + cs.stop],
                                                                               in1=Ek[:, v, :, cs], op=ALU.mult),
                                 reads=["kb", "Ek"], writes=["kstT%d" % i])
                            P.op("dve", lambda e, v=v, ref=ref, prev=prev: e.tensor_tensor(out=dec[:, v, :].unsqueeze(2), in0=ref, in1=prev, op=ALU.subtract),
                                 reads=["Ab"], writes=["dec"])
                        P.op("act", lambda e: e.activation(out=dec[:], in_=dec[:], func=AF.Exp), reads=[], writes=["dec"])
                        for v in range(4):
                            for m in range(2):
                                P.op("pe", lambda e, v=v, m=m: e.transpose(psb[:, (2 * v + m) * 128:(2 * v + m + 1) * 128], kstT[i][:, v, m, :], identb[:]),
                                     reads=["kstT%d" % i, "identb"], writes=["psb"])
                        P.op("act", lambda e: e.copy(out=kstm[i][:], in_=psb[:]), reads=["psb"], writes=["kstm0"])
                        bk = [next_ps(), next_ps()]
                        for v in range(4):
                            for m in range(2):
                                P.op("pe", lambda e, v=v, m=m: e.matmul(ps[bk[v // 2]][:, ((v % 2) * 2 + m) * 128:((v % 2) * 2 + m + 1) * 128],
                                                                        lhsT=kstm[i][:, (2 * v + m) * 128:(2 * v + m + 1) * 128],
                                                                        rhs=vC[:, n, m * 128:(m + 1) * 128], start=True, stop=True),
                                     reads=["kstm0", "vC"], writes=[("ps", bk[v // 2])])
                        own = 0 if dr == 0 else 3
                        P.op("pool", lambda e: e.tensor_copy(out=Sall[:, own, :], in_=Sbs[:]), reads=["Sbs"], writes=["Sall"])
                        for c in range(4):
                            if c == own:
                                continue
                            v = c - 1 if dr == 0 else c + 1
                            for m in range(2):
                                P.op("dve", lambda e, c=c, v=v, m=m: e.scalar_tensor_tensor(
                                    out=Sall[:, c, m * 128:(m + 1) * 128], in0=Sbs[:, m * 128:(m + 1) * 128], scalar=dec[:, v, m:m + 1],
                                    in1=ps[bk[v // 2]][:, ((v % 2) * 2 + m) * 128:((v % 2) * 2 + m + 1) * 128], op0=ALU.mult, op1=ALU.add),
                                    reads=["Sbs", "dec", ("ps", bk[v // 2])], writes=["Sall"])
                        P.op("act", lambda e: e.copy(out=Sallb[i][:], in_=Sall[:]), reads=["Sall"], writes=["Sallb%d" % i])
                        if mode == 1:
                            vn_ = 3 if dr == 0 else 0
                            for m in range(2):
                                P.op("dve", lambda e, m=m: e.scalar_tensor_tensor(
                                    out=Sbs[:, m * 128:(m + 1) * 128], in0=Sbs[:, m * 128:(m + 1) * 128], scalar=dec[:, vn_, m:m + 1],
                                    in1=ps[bk[vn_ // 2]][:, ((vn_ % 2) * 2 + m) * 128:((vn_ % 2) * 2 + m + 1) * 128], op0=ALU.mult, op1=ALU.add),
                                    reads=["dec", ("ps", bk[vn_ // 2])], writes=["Sbs"])
                            return
                        bsc = next_ps()
                        for m in range(2):
                            ms_ = slice(m * 128, (m + 1) * 128)
                            P.op("pe", lambda e, ms_=ms_: e.matmul(ps[bsc][:, ms_], lhsT=kt[i][:, ms_], rhs=qt[i][:, ms_], start=True, stop=True),
                                 reads=["kt%d" % i, "qt%d" % i], writes=[("ps", bsc)])
                        P.op("dve", lambda e: e.tensor_tensor(out=v3(PTc[i]), in0=ps[bsc][:, 0:256].rearrange("p (h t) -> p h t", h=2),
                                                              in1=gm[:, dr, :].unsqueeze(1).broadcast_to([128, 2, 128]), op=ALU.mult),
                             reads=[("ps", bsc), "gm"], writes=["PTc%d" % i])
                        bo = next_ps()
                        for m in range(2):
                            ms_ = slice(m * 128, (m + 1) * 128)
                            P.op("pe", lambda e, ms_=ms_: e.matmul(ps[bo][:, ms_], lhsT=vC[:, n, ms_], rhs=PTc[i][:, ms_], start=True, stop=False),
                                 reads=["vC", "PTc%d" % i], writes=[("ps", bo)])
                            for c in range(4):
                                cs_ = slice(m * 128 + 32 * c, m * 128 + 32 * c + 32)
                                P.op("pe", lambda e, c=c, cs_=cs_, ms_=ms_: e.matmul(ps[bo][:, cs_], lhsT=Sallb[i][:, c, ms_], rhs=qi[i][:, cs_],
                                                                                   start=False, stop=(c == 3)),
                                     reads=["Sallb%d" % i, "qi%d" % i], writes=[("ps", bo)])
                        vn = 3 if dr == 0 else 0
                        for m in range(2):
                            P.op("dve", lambda e, m=m: e.scalar_tensor_tensor(
                                out=Sbs[:, m * 128:(m + 1) * 128], in0=Sbs[:, m * 128:(m + 1) * 128], scalar=dec[:, vn, m:m + 1],
                                in1=ps[bk[vn // 2]][:, ((vn % 2) * 2 + m) * 128:((vn % 2) * 2 + m + 1) * 128], op0=ALU.mult, op1=ALU.add),
                                reads=["dec", ("ps", bk[vn // 2])], writes=["Sbs"])
                        if dr == 1:
                            P.op("act", lambda e: e.copy(out=oacc[:, :, bsl], in_=ps[bo][:, 0:256].rearrange("p (h t) -> p h t", h=2)),
                                 reads=[("ps", bo)], writes=["oacc"])
                        else:
                            P.op("dve", lambda e: e.tensor_tensor(out=v3(osum), in0=ps[bo][:, 0:256].rearrange("p (h t) -> p h t", h=2),
                                                                  in1=oacc[:, :, bsl], op=ALU.add),
                                 reads=[("ps", bo), "oacc"], writes=["osum"])
                            P.op("act", lambda e: e.activation(out=sqc[:], in_=osum[:], func=AF.Square), reads=["osum"], writes=["sqc"])
                            bm = next_ps()
                            P.op("pe", lambda e: e.matmul(ps[bm][:, 0:256], lhsT=onesC[:], rhs=sqc[:], start=True, stop=True),
                                 reads=["onesC", "sqc"], writes=[("ps", bm)])
                            P.op("dve", lambda e: e.tensor_scalar_add(out=rc_[:], in0=ps[bm][:, 0:256], scalar1=EPS), reads=[("ps", bm)], writes=["rc_"])
                            P.op("act", lambda e: e.activation(out=rc_[:], in_=rc_[:], func=AF.Sqrt), reads=[], writes=["rc_"])
                            P.op("dve", lambda e: e.reciprocal(out=rc_[:], in_=rc_[:]), reads=[], writes=["rc_"])
                            P.op("dve", lambda e: e.scalar_tensor_tensor(out=tnc[:], in0=osum[:], scalar=SC, in1=rc_[:], op0=ALU.mult, op1=ALU.mult),
                                 reads=["osum", "rc_"], writes=["tnc"])
                            for m in range(2):
                                h = h0 + m
                                P.op("dve", lambda e, m=m, h=h: e.scalar_tensor_tensor(out=mixedT[:, h, bsl], in0=tnc[:, m * 128:(m + 1) * 128],
                                                                                       scalar=hgg[:, h:h + 1], in1=mixedT[:, h, bsl],
                                                                                       op0=ALU.mult, op1=ALU.mult),
                                     reads=["tnc", "hgg"], writes=[("mix", h, n // 4)])

                    for n in (range(NB) if dr == 0 else range(NB - 1, -1, -1)):
                        block(n)
                    if mode == 1 or (emit_b and dr == 1):
                        st_fin("C", dr, Sbs[:], "Sbs", cols=slice(half * 256, (half + 1) * 256))

                if 1 in dirs:
                    dir_pass(1)
                if 0 in dirs:
                    dir_pass(0)

            half_pass(0)
            half_pass(1)


        def mixer_D():
            CDc = 12 * WG
            SC = 128.0 ** -0.5
            AR.reset()
            qhT = AR.get("qhT", [128, 4, T], BF16, nsub=4)
            khT = AR.get("khT", [128, 4, T], BF16, nsub=4)
            vtm = AR.get("vtm", [128, NB, 512], BF16)
            gall = AR.get("gall", [128, NB, 8], F32)
            bet = AR.get("bet", [128, NB, 8], F32)
            nbet = AR.get("nbet", [128, NB, 8], F32)
            cums = AR.get("cums", [128, NB, 16], F32)
            ecum = AR.get("ecum", [128, NB, 16], F32)
            sc1 = AR.get("sc1", [128, NB, 8], F32)
            csh = AR.get("csh", [128, NB, 16], BF16)
            csm = AR.get("csm", [128, NB, 16], BF16)
            csl = AR.get("csl", [128, NB, 16], BF16)
            csr = AR.get("csr", [128, NB, 16], F32)
            cTt = AR.get("cTt", [4, 256], F32)
            dtb = AR.get("dtb", [128, 8], F32)
            negA = AR.get("negA", [128, 8], F32)
            dcs = AR.get("dcs", [128, 4, 128], F32)
            onesf = AR.get("onesf", [128, 128], F32)
            ones1b = AR.get("ones1b", [128, 128], BF16)
            dng = AR.get("dng", [128, 4], F32)
            cw = AR.get("cw", [128, 12, 5], F32)
            xg = AR.get("xg", [128, 16], F32)
            mark = AR.off
            ubuf = AR.get("ubuf", [128, 2, T + 4], F32, nsub=2)
            accD = AR.get("accD", [128, T], F32)
            sraw = AR.get("sraw", [128, T], F32)
            sqD = AR.get("sqD", [128, T], BF16)
            vTt = AR.get("vTt", [128, T], BF16)
            rD = AR.get("rD", [128, 512], F32)
            conv_top = AR.off
            AR.off = mark
            brow = AR.get("brow", [128, 4, 128], F32)
            X1 = AR.get("X1", [128, 512], F32)
            X2 = AR.get("X2", [128, 512], F32)
            ErT = AR.get("ErT", [128, 4, 128], F32)
            tD = AR.get("tD", [128, 512], F32)
            Nb = [AR.get("Nb%d" % i, [128, 512], F32) for i in range(2)]
            Nt = [AR.get("Nt%d" % i, [128, 512], F32) for i in range(2)]
            IpN = AR.get("IpN", [128, 512], F32)
            Pm = [AR.get("Pm%d" % i, [128, 512], F32) for i in range(2)]
            Pbf = AR.get("Pbf", [128, 512], BF16)
            QKT = AR.get("QKT", [128, 512], BF16)
            ktm = AR.get("ktm", [128, 512], BF16)
            kbe = AR.get("kbe", [128, 512], BF16)
            kdD = AR.get("kdD", [128, 512], BF16)
            vbe = AR.get("vbe", [128, 512], BF16)
            wT0 = AR.get("wT0", [128, 4, 128], BF16)
            wT1 = AR.get("wT1", [128, 4, 128], BF16)
            qdT = AR.get("qdT", [128, 512], BF16)
            SD = AR.get("SD", [128, 4, 128], F32)
            Sb2 = [AR.get("Sb2_%d" % i, [128, 512], BF16) for i in range(2)]
            vn = AR.get("vn", [128, 512], BF16)
            ofp = AR.get("ofp", [128, 512], F32)
            oin = AR.get("oin", [128, 512], F32)
            sqo = AR.get("sqo", [128, 512], BF16)
            identb4 = AR.get("identb4", [128, 4, 128], BF16)
            AR.off = max(AR.off, conv_top)

            P.op("pool", lambda e: e.memset(onesf[:], 1.0), writes=["onesf"])
            P.op("pool", lambda e: e.memset(ones1b[:], 1.0), writes=["ones1b"])
            P.op("pool", lambda e: e.memset(wT0[:], 0.0), writes=["wT0"])
            P.op("pool", lambda e: e.memset(wT1[:], 0.0), writes=["wT1"])
            for h in range(4):
                P.op("pool", lambda e, h=h: e.tensor_copy(out=identb4[:, h, :], in_=identb[:]), reads=["identb"], writes=["identb4"])
            P.dma(dcs[:], dconst[:, :, :], writes=["dcs"])
            P.dma(dng[:], dn_norm_g[l].rearrange("(h p) -> p h", p=128), writes=["dng"], allow_slow_non_contiguous=True)
            P.dma(cw[:], dn_conv_wT[l].rearrange("(cc p) k -> p cc k", p=128), writes=["cw"])
            P.dma(dtb[:], dn_dt_bias[l].rearrange("a b -> (a b)").partition_broadcast(128), writes=["dtb"])
            P.dma(negA[:], dn_a_log[l].rearrange("a b -> (a b)").partition_broadcast(128), writes=["negA"])
            P.op("act", lambda e: e.activation(out=negA[:], in_=negA[:], func=AF.Exp), reads=[], writes=["negA"])
            P.op("dve", lambda e: e.tensor_scalar(out=negA[:], in0=negA[:], scalar1=-1.0, scalar2=None, op0=ALU.mult), reads=[], writes=["negA"])
            if mode != 1:
                inproj_fm(CDc + 3 * WG + 16, z_evac)

            def ab_evac(blk, b):
                P.op("act", lambda e: e.activation(out=bet[:, blk, :], in_=ps[b][:, 8:16], func=AF.Sigmoid), reads=[("ps", b)], writes=["bet"])
                P.op("dve", lambda e: e.tensor_tensor(out=xg[:, 0:8], in0=ps[b][:, 0:8], in1=dtb[:], op=ALU.add), reads=[("ps", b), "dtb"], writes=["xg"])
                P.op("act", lambda e: e.activation(out=xg[:, 0:8], in_=xg[:, 0:8], func=AF.Exp), reads=[], writes=["xg"])
                P.op("act", lambda e: e.activation(out=xg[:, 0:8], in_=xg[:, 0:8], func=AF.Ln, bias=1.0), reads=[], writes=["xg"])
                P.op("dve", lambda e: e.tensor_tensor(out=gall[:, blk, :], in0=xg[:, 0:8], in1=negA[:], op=ALU.mult), reads=["xg", "negA"], writes=["gall"])
            inproj_tm(CDc + 3 * WG, 16, ab_evac)
            P.op("dve", lambda e: e.tensor_scalar(out=nbet[:], in0=bet[:], scalar1=-1.0, scalar2=None, op0=ALU.mult), reads=["bet"], writes=["nbet"])

            def cum_block(n):
                b = next_ps()
                for (c0, mi, g0) in ((0, 0, 0), (4, 2, 0), (8, 1, 4), (12, 3, 4)):
                    P.op("pe", lambda e, c0=c0, mi=mi, g0=g0: e.matmul(ps[b][:, c0:c0 + 4], lhsT=dcs[:, mi, :], rhs=gall[:, n, g0:g0 + 4], start=True, stop=True),
                         reads=["dcs", "gall"], writes=[("ps", b)])
                P.op("dve", lambda e: e.tensor_copy(out=cums[:, n, :], in_=ps[b][:, 0:16]), reads=[("ps", b)], writes=["cums"])
                b2 = next_ps()
                P.op("pe", lambda e: e.matmul(ps[b2][0:4, 0:128], lhsT=gall[:, n, 0:4], rhs=dcs[:, 0, :], start=True, stop=True),
                     reads=["dcs", "gall"], writes=[("ps", b2)])
                P.op("pe", lambda e: e.matmul(ps[b2][0:4, 128:256], lhsT=gall[:, n, 4:8], rhs=dcs[:, 1, :], start=True, stop=True),
                     reads=["dcs", "gall"], writes=[("ps", b2)])
                P.op("act", lambda e: e.copy(out=cTt[:], in_=ps[b2][0:4, 0:256]), reads=[("ps", b2)], writes=["cTt"])
                P.dma(cum_scr[s % 2, n].rearrange("d h j -> h d j"), cTt[:].rearrange("h (d j) -> h d j", d=2), reads=["cTt"], writes=[("cumscr", n)])
            for n in range(NB):
                cum_block(n)
            P.op("act", lambda e: e.activation(out=ecum[:], in_=cums[:], func=AF.Exp), reads=["cums"], writes=["ecum"])
            P.op("dve", lambda e: e.tensor_copy(out=csh[:], in_=cums[:]), reads=["cums"], writes=["csh"])
            P.op("dve", lambda e: e.tensor_tensor(out=csr[:], in0=cums[:], in1=csh[:], op=ALU.subtract), reads=["cums", "csh"], writes=["csr"])
            P.op("dve", lambda e: e.tensor_copy(out=csm[:], in_=csr[:]), reads=["csr"], writes=["csm"])
            P.op("dve", lambda e: e.tensor_tensor(out=csr[:], in0=csr[:], in1=csm[:], op=ALU.subtract), reads=["csm"], writes=["csr"])
            P.op("dve", lambda e: e.tensor_copy(out=csl[:], in_=csr[:]), reads=["csr"], writes=["csl"])
            P.op("dve", lambda e: e.tensor_tensor(out=sc1[:, :, 0:4], in0=bet[:, :, 0:4], in1=ecum[:, :, 0:4], op=ALU.mult), reads=["bet", "ecum"], writes=["sc1"])
            P.op("dve", lambda e: e.tensor_tensor(out=sc1[:, :, 4:8], in0=bet[:, :, 4:8], in1=ecum[:, :, 8:12], op=ALU.mult), reads=["bet", "ecum"], writes=["sc1"])

            def qkv_group(gi6):
                gi, half = gi6 // 2, gi6 % 2

                def u_evac(m, tt, b):
                    P.op("act", lambda e: e.copy(out=ubuf[:, m, 2 + tt * 512:2 + (tt + 1) * 512], in_=ps[b][:, :]), reads=[("ps", b)], writes=[("ubuf", m)])
                for m in range(2):
                    P.op("pool", lambda e, m=m: e.memset(ubuf[:, m, 0:2], 0.0), writes=[("ubuf", m)])
                    P.op("pool", lambda e, m=m: e.memset(ubuf[:, m, T + 2:T + 4], 0.0), writes=[("ubuf", m)])
                def u_halo(m, b):
                    if hvalid[0]:
                        hcopy(ubuf[:, m, 0:2], ps[b][:, 13:15], 0, ("ubuf", m), b)
                    if hvalid[1]:
                        hcopy(ubuf[:, m, T + 2:T + 4], ps[b][:, 16:18], 1, ("ubuf", m), b)
                inproj_fm(CDc + gi * WG + half * 256, u_evac, ncols=256, halo_evac=u_halo)

                def chunk(m):
                    hd = half * 2 + m
                    cc = gi * 4 + hd
                    P.op("dve", lambda e: e.tensor_scalar(out=accD[:], in0=ubuf[:, m, 0:T], scalar1=cw[:, cc, 0:1], scalar2=None, op0=ALU.mult),
                         reads=[("ubuf", m), "cw"], writes=["accD"])
                    for k in range(1, 5):
                        P.op("dve", lambda e, k=k: e.scalar_tensor_tensor(out=accD[:], in0=ubuf[:, m, k:k + T], scalar=cw[:, cc, k:k + 1], in1=accD[:],
                                                                          op0=ALU.mult, op1=ALU.add),
                             reads=[("ubuf", m), "cw"], writes=["accD"])
                    if gi == 2:
                        P.op("act", lambda e: e.activation(out=vTt[:], in_=accD[:], func=AF.Silu), reads=["accD"], writes=["vTt"])
                        for n in range(NB):
                            P.op("pe", lambda e, n=n: e.transpose(psb[:, (n % 8) * 128:(n % 8 + 1) * 128], vTt[:, n * 128:(n + 1) * 128], identb[:]),
                                 reads=["vTt", "identb"], writes=["psb"])
                            if n % 8 == 7 or n == NB - 1:
                                n0 = (n // 8) * 8
                                cnt_ = n - n0 + 1
                                P.op("act", lambda e, n0=n0, cnt_=cnt_: e.copy(out=vtm[:, n0:n0 + cnt_, hd * 128:(hd + 1) * 128],
                                                                                in_=psb[:, 0:cnt_ * 128].rearrange("p (n t) -> p n t", n=cnt_)),
                                     reads=["psb"], writes=["vtm"])
                    else:
                        dst = qhT if gi == 0 else khT
                        dn = "qhT" if gi == 0 else "khT"
                        P.op("act", lambda e: e.activation(out=sraw[:], in_=accD[:], func=AF.Silu), reads=["accD"], writes=["sraw"])
                        P.op("pool", lambda e: e.tensor_tensor(out=sqD[:], in0=sraw[:], in1=sraw[:], op=ALU.mult), reads=["sraw"], writes=["sqD"])
                        for tt in range(NT):
                            sl = slice(tt * 512, (tt + 1) * 512)
                            b = next_ps()
                            P.op("pe", lambda e, b=b, sl=sl: e.matmul(ps[b][:, :], lhsT=ones1b[:], rhs=sqD[:, sl], start=True, stop=True),
                                 reads=["ones1b", "sqD"], writes=[("ps", b)])
                            P.op("dve", lambda e, b=b: e.tensor_scalar_add(out=rD[:], in0=ps[b][:, :], scalar1=EPS), reads=[("ps", b)], writes=["rD"])
                            P.op("act", lambda e: e.activation(out=rD[:], in_=rD[:], func=AF.Sqrt), reads=[], writes=["rD"])
                            P.op("dve", lambda e: e.reciprocal(out=rD[:], in_=rD[:]), reads=[], writes=["rD"])
                            P.op("dve", lambda e, sl=sl: e.scalar_tensor_tensor(out=dst[:, hd, sl], in0=sraw[:, sl], scalar=(SC if gi == 0 else 1.0), in1=rD[:],
                                                                                op0=ALU.mult, op1=ALU.mult),
                                 reads=["sraw", "rD"], writes=[(dn, hd)])
                for m in range(2):
                    chunk(m)
            import os as _os
            _stage = int(_os.environ.get("D_STAGE", "9"))
            if _stage < 2:
                return
            for gi6 in (range(6) if mode != 1 else range(2, 6)):
                qkv_group(gi6)
            if _stage < 3:
                return

            def sweep(dr):
                if mode != 0:
                    st_init("D", dr, SD[:].rearrange("p h t -> p (h t)"), "SD")
                else:
                    P.op("pool", lambda e: e.memset(SD[:], 0.0), writes=["SD"])
                ci = 0 if dr == 0 else 8
                cx = 4 if dr == 0 else 12
                mN = 2 if dr == 0 else 3
                mQ = 0 if dr == 0 else 1
                lastcol = (63, 127) if dr == 0 else (0, 64)
                corder = (0, 1) if dr == 0 else (1, 0)

                def block(n):
                    bsl = slice(n * 128, (n + 1) * 128)
                    cumh = cums[:, n, ci:ci + 4]
                    hb = lambda ap: ap.unsqueeze(2).broadcast_to([128, 4, 128])
                    v3 = lambda ap: ap[:].rearrange("p (h t) -> p h t", h=4)
                    pv3 = lambda b: ps[b][:, :].rearrange("p (h t) -> p h t", h=4)
                    for h in range(4):
                        P.op("pe", lambda e, h=h: e.transpose(psb[:, h * 128:(h + 1) * 128], khT[:, h, bsl], identb[:]),
                             reads=[("khT", h), "identb"], writes=["psb"])
                    P.op("act", lambda e: e.copy(out=ktm[:], in_=psb[:, 0:512]), reads=["psb"], writes=["ktm"])
                    P.op("dve", lambda e: e.tensor_tensor(out=v3(kbe), in0=v3(ktm), in1=hb(sc1[:, n, 4 * dr:4 * dr + 4]), op=ALU.mult),
                         reads=["ktm", "sc1"], writes=["kbe"])
                    P.op("dve", lambda e: e.tensor_tensor(out=v3(kdD), in0=v3(ktm), in1=hb(ecum[:, n, cx:cx + 4]), op=ALU.mult),
                         reads=["ktm", "ecum"], writes=["kdD"])
                    P.op("dve", lambda e: e.tensor_tensor(out=v3(vbe), in0=vtm[:, n, :].rearrange("p (h t) -> p h t", h=4),
                                                           in1=hb(bet[:, n, 4 * dr:4 * dr + 4]), op=ALU.mult),
                         reads=["vtm", "bet"], writes=["vbe"])
                    _sub = float(_os.environ.get("D_SUB", "9"))
                    if _sub < 1:
                        return
                    bkk, bqk = next_ps(), next_ps()
                    for h in range(4):
                        hs = slice(h * 128, (h + 1) * 128)
                        P.op("pe", lambda e, h=h, hs=hs: e.matmul(ps[bkk][:, hs], lhsT=khT[:, h, bsl], rhs=khT[:, h, bsl], start=True, stop=True),
                             reads=[("khT", h)], writes=[("ps", bkk)])
                    for h in (range(4) if mode != 1 else ()):
                        hs = slice(h * 128, (h + 1) * 128)
                        P.op("pe", lambda e, h=h, hs=hs: e.matmul(ps[bqk][:, hs], lhsT=khT[:, h, bsl], rhs=qhT[:, h, bsl], start=True, stop=True),
                             reads=[("khT", h), ("qhT", h)], writes=[("ps", bqk)])
                    if _sub < 1.5:
                        return
                    P.dma(brow[:], cum_scr[s % 2, n, dr].partition_broadcast(128), reads=[("cumscr", n)], writes=["brow"])
                    if _sub < 2:
                        return
                    P.op("dve", lambda e: e.tensor_tensor(out=v3(X1), in0=brow[:], in1=hb(cumh), op=ALU.subtract), reads=["brow", "cums"], writes=["X1"])
                    P.op("act", lambda e: e.activation(out=ErT[:], in_=brow[:], func=AF.Exp), reads=["brow"], writes=["ErT"])
                    P.op("dve", lambda e: e.tensor_scalar(out=X2[:], in0=X1[:], scalar1=-1.0, scalar2=0.0, op0=ALU.mult, op1=ALU.min), reads=["X1"], writes=["X2"])
                    P.op("dve", lambda e: e.tensor_scalar_min(out=X1[:], in0=X1[:], scalar1=0.0), reads=[], writes=["X1"])
                    P.op("act", lambda e: e.activation(out=X2[:], in_=X2[:], func=AF.Exp), reads=[], writes=["X2"])
                    P.op("act", lambda e: e.activation(out=X1[:], in_=X1[:], func=AF.Exp), reads=[], writes=["X1"])
                    P.op("dve", lambda e: e.tensor_tensor(out=tD[:], in0=ps[bkk][:, :], in1=X2[:], op=ALU.mult), reads=[("ps", bkk), "X2"], writes=["tD"])
                    P.op("dve", lambda e: e.tensor_tensor(out=v3(tD), in0=v3(tD), in1=hb(nbet[:, n, 4 * dr:4 * dr + 4]), op=ALU.mult), reads=["nbet"], writes=["tD"])
                    P.op("dve", lambda e: e.tensor_tensor(out=v3(Nb[0]), in0=v3(tD), in1=dcs[:, mN, :].unsqueeze(1).broadcast_to([128, 4, 128]), op=ALU.mult),
                         reads=["tD", "dcs"], writes=["Nb0"])
                    if mode != 1:
                        P.op("dve", lambda e: e.tensor_tensor(out=tD[:], in0=ps[bqk][:, :], in1=X1[:], op=ALU.mult), reads=[("ps", bqk), "X1"], writes=["tD"])
                        P.op("dve", lambda e: e.tensor_tensor(out=v3(QKT), in0=v3(tD), in1=dcs[:, mQ, :].unsqueeze(1).broadcast_to([128, 4, 128]), op=ALU.mult),
                             reads=["tD", "dcs"], writes=["QKT"])
                        P.op("pool", lambda e: e.tensor_tensor(out=v3(qdT), in0=qhT[:, :, bsl], in1=ErT[:], op=ALU.mult), reads=["qhT", "ErT"], writes=["qdT"])
                    if _sub < 3:
                        return
                    identf4 = ident[:].unsqueeze(1).broadcast_to([128, 4, 128])
                    btr = next_ps()
                    for h in range(4):
                        P.op("pe", lambda e, h=h: e.transpose(ps[btr][:, h * 128:(h + 1) * 128], Nb[0][:, h * 128:(h + 1) * 128], ident[:]),
                             reads=["Nb0", "ident"], writes=[("ps", btr)])
                    P.op("act", lambda e: e.copy(out=Nt[0][:], in_=ps[btr][:, :]), reads=[("ps", btr)], writes=["Nt0"])
                    P.op("dve", lambda e: e.tensor_tensor(out=v3(Pm[0]), in0=v3(Nt[0]), in1=identf4, op=ALU.add), reads=["Nt0", "ident"], writes=["Pm0"])
                    cur = 0
                    for lev in range(1, 6):
                        nx = 1 - cur
                        b1 = next_ps()
                        for h in range(4):
                            hs = slice(h * 128, (h + 1) * 128)
                            P.op("pe", lambda e, hs=hs, cur=cur, b1=b1: e.matmul(ps[b1][:, hs], lhsT=Nt[cur][:, hs], rhs=Nb[cur][:, hs], start=True, stop=True),
                                 reads=["Nt%d" % cur, "Nb%d" % cur], writes=[("ps", b1)])
                        if lev < 5:
                            b2 = next_ps()
                            for h in range(4):
                                hs = slice(h * 128, (h + 1) * 128)
                                P.op("pe", lambda e, hs=hs, cur=cur, b2=b2: e.matmul(ps[b2][:, hs], lhsT=Nb[cur][:, hs], rhs=Nt[cur][:, hs], start=True, stop=True),
                                     reads=["Nt%d" % cur, "Nb%d" % cur], writes=[("ps", b2)])
                        P.op("act", lambda e, nx=nx, b1=b1: e.copy(out=Nb[nx][:], in_=ps[b1][:, :]), reads=[("ps", b1)], writes=["Nb%d" % nx])
                        P.op("dve", lambda e, nx=nx: e.tensor_tensor(out=v3(IpN), in0=v3(Nb[nx]), in1=identf4, op=ALU.add),
                             reads=["Nb%d" % nx, "ident"], writes=["IpN"])
                        if lev < 5:
                            P.op("dve", lambda e, nx=nx, b2=b2: e.tensor_copy(out=Nt[nx][:], in_=ps[b2][:, :]), reads=[("ps", b2)], writes=["Nt%d" % nx])
                        b3 = next_ps()
                        for h in range(4):
                            hs = slice(h * 128, (h + 1) * 128)
                            P.op("pe", lambda e, hs=hs, cur=cur, b3=b3: e.matmul(ps[b3][:, hs], lhsT=IpN[:, hs], rhs=Pm[cur][:, hs], start=True, stop=True),
                                 reads=["IpN", "Pm%d" % cur], writes=[("ps", b3)])
                        P.op("dve", lambda e, nx=nx, b3=b3: e.tensor_copy(out=Pm[nx][:], in_=ps[b3][:, :]), reads=[("ps", b3)], writes=["Pm%d" % nx])
                        cur = nx
                    P.op("act", lambda e, cur=cur: e.copy(out=Pbf[:], in_=Pm[cur][:]), reads=["Pm%d" % cur], writes=["Pbf"])
                    PT_ = Pbf
                    pname = "Pbf"
                    if _sub < 4:
                        return
                    bw = next_ps()
                    for h in range(4):
                        hs = slice(h * 128, (h + 1) * 128)
                        P.op("pe", lambda e, hs=hs: e.matmul(ps[bw][:, hs], lhsT=kbe[:, hs], rhs=PT_[:, hs], start=True, stop=True),
                             reads=["kbe", pname], writes=[("ps", bw)])
                    P.op("act", lambda e: e.mul(out=wT0[:, :, 0:64], in_=pv3(bw)[:, :, 0:64], mul=-1.0), reads=[("ps", bw)], writes=["wT0"])
                    P.op("act", lambda e: e.mul(out=wT1[:, :, 64:128], in_=pv3(bw)[:, :, 64:128], mul=-1.0), reads=[("ps", bw)], writes=["wT1"])
                    if _sub < 5:
                        return
                    wTs = (wT0, wT1)
                    for k_, c in enumerate(corder):
                        rows = slice(64 * c, 64 * c + 64)
                        bv = next_ps()
                        for h in range(4):
                            hs = slice(h * 128, (h + 1) * 128)
                            P.op("pe", lambda e, hs=hs, h=h, bv=bv: e.matmul(ps[bv][:, hs], lhsT=PT_[:, hs], rhs=vbe[:, hs], start=(h == 0), stop=False),
                                 reads=[pname, "vbe"], writes=[("ps", bv)])
                        P.op("act", lambda e, c=c: e.copy(out=Sb2[c][:], in_=SD[:].rearrange("p h t -> p (h t)")), reads=["SD"], writes=["Sb2_%d" % c])
                        for h in range(4):
                            hs = slice(h * 128, (h + 1) * 128)
                            P.op("pe", lambda e, h=h, hs=hs, c=c, bv=bv: e.matmul(ps[bv][:, hs], lhsT=wTs[c][:, h, :], rhs=Sb2[c][:, hs], start=False, stop=(h == 3)),
                                 reads=["wT%d" % c, "Sb2_%d" % c], writes=[("ps", bv)])
                        P.op("dve", lambda e, rows=rows, bv=bv: e.tensor_copy(out=vn[rows, :], in_=ps[bv][rows, :]), reads=[("ps", bv)], writes=["vn"])
                        bk = next_ps()
                        for h in range(4):
                            hs = slice(h * 128, (h + 1) * 128)
                            P.op("pe", lambda e, hs=hs, rows=rows, bk=bk: e.matmul(ps[bk][:, hs], lhsT=kdD[rows, hs], rhs=vn[rows, hs], start=True, stop=True),
                                 reads=["kdD", "vn"], writes=[("ps", bk)])
                        lc = lastcol[c]
                        P.op("dve", lambda e, lc=lc: e.tensor_tensor(out=SD[:], in0=SD[:], in1=ErT[:, :, lc:lc + 1].broadcast_to([128, 4, 128]), op=ALU.mult),
                             reads=["ErT"], writes=["SD"])
                        P.op("dve", lambda e, bk=bk: e.tensor_tensor(out=SD[:], in0=SD[:], in1=pv3(bk), op=ALU.add), reads=[("ps", bk)], writes=["SD"])
                    if _sub < 6:
                        return
                    if mode == 1:
                        return
                    bo = next_ps()
                    for h in range(4):
                        hs = slice(h * 128, (h + 1) * 128)
                        P.op("pe", lambda e, hs=hs: e.matmul(ps[bo][:, hs], lhsT=vn[:, hs], rhs=QKT[:, hs], start=True, stop=False),
                             reads=["vn", "QKT"], writes=[("ps", bo)])
                        for c in range(2):
                            cs_ = slice(h * 128 + 64 * c, h * 128 + 64 * c + 64)
                            P.op("pe", lambda e, hs=hs, cs_=cs_, c=c: e.matmul(ps[bo][:, cs_], lhsT=Sb2[c][:, hs], rhs=qdT[:, cs_], start=False, stop=(c == 1)),
                                 reads=["Sb2_%d" % c, "qdT"], writes=[("ps", bo)])
                    if dr == 0:
                        P.op("act", lambda e: e.copy(out=ofp[:], in_=ps[bo][:, :]), reads=[("ps", bo)], writes=["ofp"])
                        P.dma(d_scr[s % 2, n], ofp[:], reads=["ofp"], writes=[("dscr", n)])
                    else:
                        P.dma(oin[:], d_scr[s % 2, n], reads=[("dscr", n)], writes=["oin"])
                        P.op("dve", lambda e: e.tensor_tensor(out=ofp[:], in0=ps[bo][:, :], in1=oin[:], op=ALU.add), reads=[("ps", bo), "oin"], writes=["ofp"])
                        P.op("act", lambda e: e.activation(out=sqo[:], in_=ofp[:], func=AF.Square), reads=["ofp"], writes=["sqo"])
                        bm = next_ps()
                        P.op("pe", lambda e: e.matmul(ps[bm][:, :], lhsT=onesh[:], rhs=sqo[:], start=True, stop=True), reads=["onesh", "sqo"], writes=[("ps", bm)])
                        P.op("dve", lambda e: e.tensor_scalar_add(out=oin[:], in0=ps[bm][:, :], scalar1=EPS), reads=[("ps", bm)], writes=["oin"])
                        P.op("act", lambda e: e.activation(out=oin[:], in_=oin[:], func=AF.Sqrt), reads=[], writes=["oin"])
                        P.op("dve", lambda e: e.reciprocal(out=oin[:], in_=oin[:]), reads=[], writes=["oin"])
                        P.op("dve", lambda e: e.tensor_tensor(out=ofp[:], in0=ofp[:], in1=oin[:], op=ALU.mult), reads=["oin"], writes=["ofp"])
                        for h in range(4):
                            hs = slice(h * 128, (h + 1) * 128)
                            P.op("dve", lambda e, h=h, hs=hs: e.scalar_tensor_tensor(out=mixedT[:, h, bsl], in0=ofp[:, hs], scalar=dng[:, h:h + 1],
                                                                                     in1=mixedT[:, h, bsl], op0=ALU.mult, op1=ALU.mult),
                                 reads=["ofp", "dng"], writes=[("mix", h, n // 4)])

                for n in (range(NB) if dr == 0 else range(NB - 1, -1, -1)):
                    block(n)
                if mode == 1 or (emit_b and dr == 1):
                    st_fin("D", dr, SD[:].rearrange("p h t -> p (h t)"), "SD")

            if 0 in dirs:
                sweep(0)
            if 1 in dirs:
                sweep(1)

        mixers = {"A": (mixer_A, 0), "B": (mixer_B, 1), "C": (mixer_C, 2), "D": (mixer_D, 3)}
        order = [m for m in "ABCD" if m in MIXERS]
        if mode == 1:
            order = [m for m in order if m != "A"]
        for k, m in enumerate(order):
            fn, mi = mixers[m]
            fn()
            if mode != 1:
                outproj(mi, k == 0, k == len(order) - 1)

    MXI = {"B": 0, "C": 1, "D": 2}

    def select_states():
        AR.reset()
        gath = AR.get("gath", [128, NRP, 512], F32)
        ohs = AR.get("ohs", [128, 8], F32)
        Si = AR.get("Si", [128, 512], F32)
        P.dma(ohs[:], oh_in.partition_broadcast(128), writes=["ohs"])
        for mx in [m_ for m_ in "BCD" if m_ in MIXERS]:
            for dr in range(2):
                slot = MXI[mx] * 2 + dr
                j0 = 0 if dr == 0 else 1
                P.dma(gath[:], sv[dr, j0:j0 + NRP, MXI[mx]].rearrange("r p w -> p r w"),
                      reads=[("sv", dr, j0 + r) for r in range(NRP)], writes=["gath"])
                P.op("pool", lambda e: e.memset(Si[:], 0.0), writes=["Si"])
                for r in range(NRP):
                    P.op("dve", lambda e, r=r: e.scalar_tensor_tensor(out=Si[:], in0=gath[:, r, :], scalar=ohs[:, r:r + 1], in1=Si[:], op0=ALU.mult, op1=ALU.add),
                         reads=["gath", "ohs"], writes=["Si"])
                P.dma(sin_scr[slot], Si[:], reads=["Si"], writes=[("sin", slot)])

    def run_seg(s, l, mode=0, pj=None):
        last = (l == LAYERS - 1)
        src_ap = x_in[s] if l == 0 else x_scr[s]
        seg_layer(s, l, src_ap, y_out[s] if last else x_scr[s], last, mode=mode, pj=pj)
        P.state[("x", s)] = P.state.get(("xd", s, l), [None, []])

    samples = [s for s in range(NSEG) if s != PSEG]
    if NR and PSEG is not None:
        AR.reset()
        zt = AR.get("zt", [128, 512], F32)
        P.op("pool", lambda e: e.memset(zt[:], 0.0), writes=["zt"])

        class _AnyOf:
            pass

        def static_halo(l, j):
            X = xp_full if l == 0 else xp1
            return dict(kind="static", left=(X[j - 1, T - 15:T, :] if j > 0 else None), right=(X[j + 1, 0:15, :] if j < NRP - 1 else None),
                        lkey=("xp", l, j - 1), rkey=("xp", l, j + 1))

        def chain_states(l):
            for mxi in range(3):
                P.dma(sv[0, 0, mxi], zt[:], reads=["zt"], writes=[("sv", 0, 0)])
                P.dma(sv[1, NRP, mxi], zt[:], reads=["zt"], writes=[("sv", 1, NRP)])
            for dr in ((0, 1) if l == LAYERS - 1 else (0,)):
                order_ = range(NRP) if dr == 0 else range(NRP - 1, -1, -1)
                for j in order_:
                    jin, jout = (j, j + 1) if dr == 0 else (j + 1, j)
                    pj = dict(rope=ropeP[j], dirs=(dr,), xkey=("xp", l, j),
                              init={(mx, dr): (sv[dr, jin, MXI[mx]], ("sv", dr, jin)) for mx in "BCD"},
                              fin={(mx, dr): (sv[dr, jout, MXI[mx]], ("sv", dr, jout)) for mx in "BCD"},
                              halo=static_halo(l, j))
                    x_src = xp_full[j] if l == 0 else xp1[j]
                    seg_layer(PSEG, l, x_src, None, False, mode=1, pj=pj)

        def full_redundant(l):
            for j in range(NRP - 1, -1, -1):
                pj = dict(rope=ropeP[j], xkey=("xp", l, j), xdkey=("xp", l + 1, j),
                          init={(mx, dr): (sv[dr, j + dr, MXI[mx]], ("sv", dr, j + dr)) for mx in "BCD" for dr in (0, 1)},
                          fin={(mx, 1): (sv[1, j, MXI[mx]], ("sv", 1, j)) for mx in "BCD"},
                          halo=static_halo(l, j))
                x_src = xp_full[j] if l == 0 else xp1[j]
                seg_layer(PSEG, l, x_src, xp1[j], False, mode=2, pj=pj)

        for l in range(LAYERS):
            chain_states(l)
            if l < LAYERS - 1:
                full_redundant(l)
            select_states()
            pj = dict(init={(mx, dr): (sin_scr[MXI[mx] * 2 + dr], ("sin", MXI[mx] * 2 + dr)) for mx in "BCD" for dr in (0, 1)},
                      halo=dict(kind="sel", X=(xp_full if l == 0 else xp1), keys=[("xp", l, j_) for j_ in range(NRP)]))
            run_seg(PSEG, l, mode=2, pj=pj)
        for s in samples:
            for l2 in range(LAYERS):
                run_seg(s, l2)
    else:
        for s in range(NSEG):
            for l in range(LAYERS):
                run_seg(s, l)

    P.emit(sems, dsems)
    for sem, v in P.final_dma:
        nc.sync.wait_ge(sem, v)
    es.close()
    return nc


def make_common(inputs):
    f = lambda a: np.ascontiguousarray(np.asarray(a, dtype=np.float32))
    return {
        "ada_w": f(inputs["ada_w"]), "ada_b": f(inputs["ada_b"]), "norm_g": f(inputs["norm_g"]),
        "w_in": f(inputs["w_in"]), "w_out": f(inputs["w_out"]),
        "conv_a_wT": f(np.transpose(np.asarray(inputs["conv_a_w"]), (0, 2, 1))),
        "conv_a_b": f(inputs["conv_a_b"]), "ln_a_g": f(inputs["ln_a_g"]), "ln_a_b": f(inputs["ln_a_b"]),
        "final_g": f(inputs["final_g"]), "ident": np.eye(128, dtype=np.float32),
        "ret_norm_g": f(inputs["ret_norm_g"]), "retc": ret_consts(),
        "hgrn_norm_g": f(inputs["hgrn_norm_g"]), "hgrn_lb_logits": f(inputs["hgrn_lb_logits"]), "gmask": gla_masks(),
        "dn_conv_wT": f(np.transpose(np.asarray(inputs["dn_conv_w"]), (0, 2, 1))), "dn_a_log": f(inputs["dn_a_log"]),
        "dn_dt_bias": f(inputs["dn_dt_bias"]), "dn_norm_g": f(inputs["dn_norm_g"]), "dconst": dn_masks(),
        "oh": np.zeros(8, np.float32), "retT": ret_seg_decay(2048),
    }


def kernel(**inputs):
    T, NSEG = 2048, 5
    xp = np.asarray(inputs["x_prompt"], dtype=np.float32)
    xs = np.asarray(inputs["x_sample"], dtype=np.float32)
    cp = np.asarray(inputs["c_prompt"], dtype=np.float32)
    cs = np.asarray(inputs["c_sample"], dtype=np.float32)
    common = make_common(inputs)
    nc = build(T, NSEG, rope_idx=[0, 0, 0, 0, 1], NR=NCORES, PSEG=4)
    xp_full = np.ascontiguousarray(xp[0].reshape(NCORES, T, D))
    ropeP = np.ascontiguousarray(np.stack([rope_tables(j * T, T) for j in range(NCORES)], axis=0))
    in_maps = []
    for c in range(NCORES):
        x_in = np.concatenate([xs[4 * c:4 * c + 4], xp[:, c * T:(c + 1) * T]], axis=0)
        cc = np.concatenate([cs[4 * c:4 * c + 4], cp], axis=0)
        m = dict(common)
        m["x_in"] = np.ascontiguousarray(x_in)
        m["cT"] = np.ascontiguousarray(cc.T)
        m["rope"] = np.ascontiguousarray(np.stack([rope_tables(0, T), rope_tables(c * T, T)], axis=0))
        oh = np.zeros(8, np.float32)
        oh[c] = 1.0
        m["oh"] = oh
        m["xp_full"] = xp_full
        ohlr = np.zeros((2, 8), np.float32)
        if c > 0:
            ohlr[0, c - 1] = 1.0
        if c < NCORES - 1:
            ohlr[1, c + 1] = 1.0
        m["ohLR"] = ohlr
        m["ropeP"] = ropeP
        in_maps.append(m)
    res = run_bass_kernel_spmd(nc, in_maps, core_ids=list(range(NCORES)))
    y_p = np.zeros_like(xp)
    y_s = np.zeros_like(xs)
    for c in range(NCORES):
        y = np.asarray(res.results[c]["y_out"])
        y_s[4 * c:4 * c + 4] = y[0:4]
        y_p[0, c * T:(c + 1) * T] = y[4]
    return (y_p, y_s)
```
